# Optimizing a Trainium2 kernel written in Bass

```python
import math
import jax, jax.numpy as jnp
from jax import lax
import numpy as np

D_MODEL = 1024
BATCH = 8
SEQ = 2048
DEPTH = 2

GRID_W = 64
NA_HEADS = 16
NA_HEAD_DIM = 64
ATTN_W = NA_HEADS * NA_HEAD_DIM
WIN_R = 8
WIN_C = 16
SSD_EXPAND = 2
D_INNER = SSD_EXPAND * D_MODEL
SSD_HEAD_DIM = 64
SSD_HEADS = D_INNER // SSD_HEAD_DIM
SSD_GROUPS = 8
HEADS_PER_GROUP = SSD_HEADS // SSD_GROUPS
D_STATE = 128
CONV_K = 5
CHUNK = 128
CONV_CH = D_INNER + 2 * SSD_GROUPS * D_STATE
NORM_GROUP = D_INNER // SSD_GROUPS
D_FF = 4 * D_MODEL
N_BRANCH = 2
N_IN = 3 * ATTN_W + D_INNER + CONV_CH + 2 * SSD_HEADS + N_BRANCH * D_MODEL
DN_ALPHA = (2 * DEPTH) ** 0.25
DN_BETA = (8 * DEPTH) ** -0.25
LN_EPS = 1e-5
RMS_EPS = 1e-5

kernel_name = "hybrid_natten_ssd_deepnorm_encoder"


def layer_norm(x, g, b):
    xf = x.astype(jnp.float32)
    mu = jnp.mean(xf, axis=-1, keepdims=True)
    var = jnp.mean(jnp.square(xf - mu), axis=-1, keepdims=True)
    return ((xf - mu) * lax.rsqrt(var + LN_EPS) * g + b).astype(x.dtype)


def neighbourhood_attention(q, k, v, rpb):
    bsz, s, h, dh = q.shape
    rows = s // GRID_W
    kr = min(WIN_R, rows)
    kc = min(WIN_C, GRID_W)
    cols = jnp.arange(GRID_W)
    col_start = jnp.clip(cols - kc // 2, 0, GRID_W - kc)
    col_idx = col_start[:, None] + jnp.arange(kc)[None, :]
    dc = col_idx - cols[:, None] + (WIN_C - 1)
    qg = jnp.moveaxis(q.reshape(bsz, rows, GRID_W, h, dh), 1, 0) * (dh ** -0.5)
    kg = k.reshape(bsz, rows, GRID_W, h, dh)
    vg = v.reshape(bsz, rows, GRID_W, h, dh)

    def row_block(args):
        r, q_row = args
        r0 = jnp.clip(r - kr // 2, 0, rows - kr)
        k_rows = lax.dynamic_slice_in_dim(kg, r0, kr, axis=1)
        v_rows = lax.dynamic_slice_in_dim(vg, r0, kr, axis=1)
        k_nb = k_rows[:, :, col_idx]
        v_nb = v_rows[:, :, col_idx]
        dr = r0 + jnp.arange(kr) - r + (WIN_R - 1)
        bias = rpb[:, dr[None, :, None], dc[:, None, :]].astype(jnp.float32)
        sc = jnp.einsum('bwhd,brwkhd->bhwrk', q_row, k_nb).astype(jnp.float32) + bias[None]
        p = jax.nn.softmax(sc.reshape(bsz, h, GRID_W, kr * kc), axis=-1)
        p = p.reshape(sc.shape).astype(v.dtype)
        return jnp.einsum('bhwrk,brwkhd->bwhd', p, v_nb)

    out = lax.map(row_block, (jnp.arange(rows), qg))
    return jnp.moveaxis(out, 0, 1).reshape(bsz, s, h * dh)


def centred_dwconv(u, w, b):
    pad = CONV_K // 2
    y = lax.conv_general_dilated(u, w[:, None, :], window_strides=(1,), padding=[(pad, pad)],
                                 dimension_numbers=('NWC', 'WIO', 'NWC'),
                                 feature_group_count=u.shape[-1])
    return y + b


def ssd_chunked(xh, dt, a, bmat, cmat):
    bsz, s, g, r, p = xh.shape
    n = bmat.shape[-1]
    nc = s // CHUNK
    xc = (xh * dt[..., None]).reshape(bsz, nc, CHUNK, g, r, p)
    adt = (dt * a).reshape(bsz, nc, CHUNK, g, r).transpose(0, 1, 3, 4, 2)
    a_cs = jnp.cumsum(adt, axis=-1)
    bc = bmat.reshape(bsz, nc, CHUNK, g, n)
    cc = cmat.reshape(bsz, nc, CHUNK, g, n)
    seg = a_cs[..., :, None] - a_cs[..., None, :]
    tril = jnp.tril(jnp.ones((CHUNK, CHUNK), dtype=bool))
    lmat = jnp.exp(jnp.where(tril, seg, -jnp.inf))
    cb = jnp.einsum('bclgn,bcsgn->bcgls', cc, bc)
    y_diag = jnp.einsum('bcgls,bcgrls,bcsgrp->bclgrp', cb, lmat, xc)
    decay_states = jnp.exp(a_cs[..., -1:] - a_cs)
    states = jnp.einsum('bclgn,bcgrl,bclgrp->bcgrpn', bc, decay_states, xc)
    chunk_decay = jnp.exp(a_cs[..., -1])

    def step(hs, inp):
        st, dec = inp
        return hs * dec[..., None, None] + st, hs

    h0 = jnp.zeros((bsz, g, r, p, n), dtype=states.dtype)
    _, prev = lax.scan(step, h0, (jnp.moveaxis(states, 1, 0), jnp.moveaxis(chunk_decay, 1, 0)))
    prev = jnp.moveaxis(prev, 0, 1)
    y_off = jnp.einsum('bclgn,bcgrpn,bcgrl->bclgrp', cc, prev, jnp.exp(a_cs))
    return (y_diag + y_off).reshape(bsz, s, g, r, p)


def hybrid_mixer(u, w_in, conv_w, conv_b, a_log, dt_bias, d_skip, ssd_norm_w, rpb,
                 w_attn_br, w_ssd_br, b_gate, w_o):
    bsz, s, _ = u.shape
    f32 = jnp.float32
    proj = u @ w_in
    offs = [ATTN_W, 2 * ATTN_W, 3 * ATTN_W, 3 * ATTN_W + D_INNER,
            3 * ATTN_W + D_INNER + CONV_CH, 3 * ATTN_W + D_INNER + CONV_CH + 2 * SSD_HEADS]
    q, k, v, z, xbc, dt_raw, gate = jnp.split(proj, offs, axis=-1)

    hs = (bsz, s, NA_HEADS, NA_HEAD_DIM)
    ya = neighbourhood_attention(q.reshape(hs), k.reshape(hs), v.reshape(hs), rpb) @ w_attn_br

    xbc = jax.nn.silu(centred_dwconv(xbc, conv_w, conv_b)).astype(f32)
    xs, bm, cm = jnp.split(xbc, [D_INNER, D_INNER + SSD_GROUPS * D_STATE], axis=-1)
    xs = xs.reshape(bsz, s, SSD_GROUPS, HEADS_PER_GROUP, SSD_HEAD_DIM)
    bm = bm.reshape(bsz, s, SSD_GROUPS, D_STATE)
    cm = cm.reshape(bsz, s, SSD_GROUPS, D_STATE)
    dt = jax.nn.softplus(dt_raw.astype(f32).reshape(bsz, s, 2, SSD_GROUPS, HEADS_PER_GROUP)
                         + dt_bias.astype(f32).reshape(2, SSD_GROUPS, HEADS_PER_GROUP))
    a = -jnp.exp(a_log.astype(f32)).reshape(2, SSD_GROUPS, HEADS_PER_GROUP)
    y_f = ssd_chunked(xs, dt[:, :, 0], a[0], bm, cm)
    flip = lambda t: jnp.flip(t, axis=1)
    y_b = flip(ssd_chunked(flip(xs), flip(dt[:, :, 1]), a[1], flip(bm), flip(cm)))
    y = y_f + y_b + xs * d_skip.astype(f32).reshape(SSD_GROUPS, HEADS_PER_GROUP)[..., None]
    y = y.reshape(bsz, s, SSD_GROUPS, NORM_GROUP) * jax.nn.silu(z.astype(f32)).reshape(bsz, s, SSD_GROUPS, NORM_GROUP)
    y = y * lax.rsqrt(jnp.mean(jnp.square(y), axis=-1, keepdims=True) + RMS_EPS)
    y = y * ssd_norm_w.astype(f32).reshape(SSD_GROUPS, NORM_GROUP)
    ys = y.reshape(bsz, s, D_INNER).astype(u.dtype) @ w_ssd_br

    gates = jax.nn.sigmoid((gate + b_gate).astype(f32)).reshape(bsz, s, N_BRANCH, D_MODEL).astype(u.dtype)
    merged = gates[:, :, 0] * ya + gates[:, :, 1] * ys
    return merged @ w_o


def sq_relu_mlp(x, w1, w2):
    return jnp.square(jax.nn.relu(x @ w1)) @ w2


def setup_inputs(seed: int = 0) -> dict:
    key = jax.random.key(seed)
    ks = jax.random.split(key, 24)
    L = DEPTH
    nrm = lambda k, shape, scale: jax.random.normal(k, shape, jnp.float32) * scale
    x = jax.random.normal(ks[0], (BATCH, SEQ, D_MODEL), jnp.float32)
    ln0_g = 1.0 + nrm(ks[1], (D_MODEL,), 0.02)
    ln0_b = nrm(ks[2], (D_MODEL,), 0.02)
    col_scale = jnp.concatenate([
        jnp.ones((2 * ATTN_W,), jnp.float32),
        jnp.full((ATTN_W,), DN_BETA, jnp.float32),
        jnp.ones((D_INNER,), jnp.float32),
        jnp.full((D_INNER,), DN_BETA, jnp.float32),
        jnp.ones((N_IN - 3 * ATTN_W - 2 * D_INNER,), jnp.float32)])
    w_in = nrm(ks[3], (L, D_MODEL, N_IN), D_MODEL ** -0.5) * col_scale
    conv_w = nrm(ks[4], (L, CONV_K, CONV_CH), CONV_K ** -0.5)
    conv_b = nrm(ks[5], (L, CONV_CH), 0.01)
    a_log = jnp.log(jax.random.uniform(ks[6], (L, 2, SSD_HEADS), jnp.float32, 1.0, 16.0))
    dt0 = jnp.exp(jax.random.uniform(ks[7], (L, 2, SSD_HEADS), jnp.float32, math.log(1e-3), math.log(0.1)))
    dt_bias = dt0 + jnp.log(-jnp.expm1(-dt0))
    d_skip = 1.0 + nrm(ks[8], (L, SSD_HEADS), 0.1)
    ssd_norm_w = 1.0 + nrm(ks[9], (L, D_INNER), 0.02)
    rpb = nrm(ks[10], (L, NA_HEADS, 2 * WIN_R - 1, 2 * WIN_C - 1), 0.02)
    w_attn_br = nrm(ks[11], (L, ATTN_W, D_MODEL), ATTN_W ** -0.5 * DN_BETA)
    w_ssd_br = nrm(ks[12], (L, D_INNER, D_MODEL), D_INNER ** -0.5 * DN_BETA)
    b_gate = nrm(ks[13], (L, N_BRANCH * D_MODEL), 0.1)
    w_o = nrm(ks[14], (L, D_MODEL, D_MODEL), D_MODEL ** -0.5 * DN_BETA)
    ln1_g = 1.0 + nrm(ks[15], (L, D_MODEL), 0.02)
    ln1_b = nrm(ks[16], (L, D_MODEL), 0.02)
    w_ff1 = nrm(ks[17], (L, D_MODEL, D_FF), D_MODEL ** -0.5 * DN_BETA)
    w_ff2 = nrm(ks[18], (L, D_FF, D_MODEL), D_FF ** -0.5 * DN_BETA)
    ln2_g = 1.0 + nrm(ks[19], (L, D_MODEL), 0.02)
    ln2_b = nrm(ks[20], (L, D_MODEL), 0.02)
    return {"x": x, "ln0_g": ln0_g, "ln0_b": ln0_b, "w_in": w_in, "conv_w": conv_w,
            "conv_b": conv_b, "a_log": a_log, "dt_bias": dt_bias, "d_skip": d_skip,
            "ssd_norm_w": ssd_norm_w, "rpb": rpb, "w_attn_br": w_attn_br, "w_ssd_br": w_ssd_br,
            "b_gate": b_gate, "w_o": w_o, "ln1_g": ln1_g, "ln1_b": ln1_b, "w_ff1": w_ff1,
            "w_ff2": w_ff2, "ln2_g": ln2_g, "ln2_b": ln2_b}


def reference(x, ln0_g, ln0_b, w_in, conv_w, conv_b, a_log, dt_bias, d_skip, ssd_norm_w, rpb,
              w_attn_br, w_ssd_br, b_gate, w_o, ln1_g, ln1_b, w_ff1, w_ff2, ln2_g, ln2_b):
    h = layer_norm(x, ln0_g, ln0_b)
    for l in range(DEPTH):
        mix = hybrid_mixer(h, w_in[l], conv_w[l], conv_b[l], a_log[l], dt_bias[l], d_skip[l],
                           ssd_norm_w[l], rpb[l], w_attn_br[l], w_ssd_br[l], b_gate[l], w_o[l])
        h = layer_norm(DN_ALPHA * h + mix, ln1_g[l], ln1_b[l])
        h = layer_norm(DN_ALPHA * h + sq_relu_mlp(h, w_ff1[l], w_ff2[l]), ln2_g[l], ln2_b[l])
    return h
```

```python
import numpy as np
from contextlib import ExitStack
import concourse.bass as bass
import concourse.mybir as mybir
from concourse.bass_utils import run_bass_kernel_spmd

F32 = mybir.dt.float32
BF16 = mybir.dt.bfloat16
AF = mybir.ActivationFunctionType
ALU = mybir.AluOpType
AX = mybir.AxisListType

L = 2
S_TOK = 2048
D = 1024
NT = 16
NEG = -30000.0
LN_EPS = 1e-5
RMS_EPS = 1e-5
DN_ALPHA = float((2 * L) ** 0.25)

ENGS = ("pe", "act", "dve", "pool", "sp")
NSLOT = {"sp": 8, "pool": 8, "act": 4}


class Res:
    __slots__ = ("w", "r", "rd")

    def __init__(self):
        self.w = None
        self.r = {}
        self.rd = []


class Buf:
    def __init__(self, t, nres=1):
        self.t = t
        self.r = [Res() for _ in range(nres)]

    def __getitem__(self, k):
        return self.t[k]


class Sched:
    def __init__(self, nc):
        self.nc = nc
        self.es = ExitStack()
        self.csem = {e: self.es.enter_context(nc.semaphore("c_" + e)) for e in ENGS}
        self.dsem = {}
        for e, ns in NSLOT.items():
            for s in range(ns):
                self.dsem[(e, s)] = self.es.enter_context(nc.semaphore("d_%s%d" % (e, s)))
        self.cbase = {e: 0 for e in ENGS}
        self.dcnt = {e: 0 for e in NSLOT}
        self.batch = 0
        self.ops = {e: [] for e in ENGS}
        self.nops = 0

    def add(self, eng, fn, reads=(), writes=(), dma=False):
        idx = len(self.ops[eng])
        b = self.batch
        deps = set()
        for r in reads:
            if r.w is not None and r.w[0] == b:
                deps.add(r.w[1:])
        for w in writes:
            if w.w is not None and w.w[0] == b:
                deps.add(w.w[1:])
            for e2, (b2, i2) in w.r.items():
                if b2 == b:
                    deps.add((e2, i2))
            for x in w.rd:
                if x[0] == b:
                    deps.add(x[1:])
        deps.discard((eng, idx))
        for r in reads:
            if dma:
                r.rd.append((b, eng, idx))
            else:
                r.r[eng] = (b, idx)
        for w in writes:
            w.w = (b, eng, idx)
            w.r = {}
            w.rd = []
        self.ops[eng].append({"fn": fn, "deps": deps, "dma": dma, "inc": False})
        self.nops += 1

    def flush(self):
        nc = self.nc
        ops = self.ops
        for e in ENGS:
            for op in ops[e]:
                for (e2, i2) in op["deps"]:
                    t = ops[e2][i2]
                    if t["dma"]:
                        continue
                    if e2 == e and e == "pe":
                        continue
                    t["inc"] = True
            for op in reversed(ops[e]):
                if not op["dma"]:
                    op["inc"] = True
                    break
        cnt = {}
        final_c = {}
        for e in ENGS:
            c = self.cbase[e]
            for i, op in enumerate(ops[e]):
                if op["inc"] and not op["dma"]:
                    c += 1
                cnt[(e, i)] = c
            final_c[e] = c
        dslot = {}
        final_d = {}
        for e in ENGS:
            for i, op in enumerate(ops[e]):
                if op["dma"]:
                    ns = NSLOT[e]
                    j = self.dcnt[e]
                    dslot[(e, i)] = (e, j % ns, 16 * (j // ns + 1))
                    self.dcnt[e] = j + 1
        for e, ns in NSLOT.items():
            for s in range(ns):
                n = (self.dcnt[e] - s + ns - 1) // ns if self.dcnt[e] > s else 0
                final_d[(e, s)] = 16 * n
        csem, dsem = self.csem, self.dsem

        def run(e, eng):
            seen_c = dict(self.cbase)
            seen_d = {}
            for i, op in enumerate(ops[e]):
                waits_c = {}
                waits_d = {}
                for (e2, i2) in op["deps"]:
                    t = ops[e2][i2]
                    if t["dma"]:
                        se, sl, val = dslot[(e2, i2)]
                        k = (se, sl)
                        if seen_d.get(k, 0) < val:
                            waits_d[k] = max(waits_d.get(k, 0), val)
                    else:
                        if e2 == e and e == "pe":
                            continue
                        v = cnt[(e2, i2)]
                        if seen_c[e2] < v:
                            waits_c[e2] = max(waits_c.get(e2, 0), v)
                if op["dma"]:
                    se, sl, val = dslot[(e, i)]
                    if val > 16:
                        k = (se, sl)
                        if seen_d.get(k, 0) < val - 16:
                            waits_d[k] = max(waits_d.get(k, 0), val - 16)
                for e2, v in waits_c.items():
                    eng.wait_ge(csem[e2], v)
                    seen_c[e2] = v
                for k, v in waits_d.items():
                    eng.wait_ge(dsem[k], v)
                    seen_d[k] = v
                ins = op["fn"](eng)
                if op["dma"]:
                    se, sl, val = dslot[(e, i)]
                    ins.then_inc(dsem[(se, sl)], 16)
                elif op["inc"]:
                    ins.then_inc(csem[e], 1)
            for e2 in ENGS:
                if e2 != e and final_c[e2] > seen_c[e2]:
                    eng.wait_ge(csem[e2], final_c[e2])
            for k, v in final_d.items():
                if v > 0 and seen_d.get(k, 0) < v:
                    eng.wait_ge(dsem[k], v)

        with nc.Block() as block:
            @block.tensor
            def _(eng):
                run("pe", eng)

            @block.scalar
            def _(eng):
                run("act", eng)

            @block.vector
            def _(eng):
                run("dve", eng)

            @block.gpsimd
            def _(eng):
                run("pool", eng)

            @block.sync
            def _(eng):
                run("sp", eng)

        self.cbase = final_c
        self.batch += 1
        self.ops = {e: [] for e in ENGS}


def _rpb_tables(rpb_l):
    kc = np.arange(64)[:, None]
    qc = np.arange(64)[None, :]
    c0 = np.clip(qc - 8, 0, 48)
    colvalid = (kc >= c0) & (kc < c0 + 16)
    dcc = np.clip(kc - qc + 15, 0, 30)
    T = rpb_l[:, :, dcc]
    T = np.where(colvalid[None, None], T, np.float32(NEG)).astype(np.float32)
    out = np.full((16, 128, 2, 15, 64), NEG, np.float32)
    for var in range(2):
        for pos in range(15):
            dr = 14 - pos
            if var == 0 or 3 <= dr <= 10:
                out[:, 0:64, var, pos, :] = T[:, dr]
            dr1 = dr + 1
            if dr1 <= 14 and (var == 0 or 3 <= dr1 <= 10):
                out[:, 64:128, var, pos, :] = T[:, dr1]
    return out.reshape(16, 128, 2 * 15 * 64)


def _kmajor(w, kc):
    n = w.shape[1]
    return np.ascontiguousarray(w.reshape(kc, 128, n).transpose(1, 0, 2).reshape(128, kc * n))


def prep_shared(inp):
    f = np.float32
    o = {}
    o["lnp"] = np.stack([inp["ln0_g"], inp["ln0_b"]] +
                        [inp[k][l] for l in range(L) for k in ("ln1_g", "ln1_b", "ln2_g", "ln2_b")]).astype(f)
    j = np.arange(128)[:, None]
    l_ = np.arange(128)[None, :]
    o["c_ident"] = np.eye(128, dtype=f)
    o["c_uf"] = (j <= l_).astype(f)
    o["c_ub"] = (j >= l_).astype(f)
    o["c_mf"] = np.where(l_ >= j, 0.0, NEG).astype(f)
    o["c_mb"] = np.where(l_ <= j, 0.0, NEG).astype(f)
    sel = np.zeros((32, 32, 128), f)
    for r in range(32):
        sel[r, r, :] = 1.0
    o["c_sel"] = np.concatenate([sel.reshape(32, 32 * 128)] * 2, axis=0)
    last = np.zeros((128, 128), f)
    last[127, :] = 1.0
    first = np.zeros((128, 128), f)
    first[0, :] = 1.0
    o["c_last"] = last
    o["c_first"] = first
    w_in = inp["w_in"]
    wq, wg, wdt, wgate, rp = [], [], [], [], []
    for l in range(L):
        wl = w_in[l]
        a = []
        for hp in range(8):
            cols = np.concatenate([np.arange(hp * 128, hp * 128 + 128), 1024 + np.arange(hp * 128, hp * 128 + 128),
                                   2048 + np.arange(hp * 128, hp * 128 + 128)])
            a.append(_kmajor(wl[:, cols], 8))
        wq.append(np.stack(a))
        a = []
        for g in range(8):
            cols = np.concatenate([3072 + g * 256 + np.arange(256), 5120 + g * 256 + np.arange(256),
                                   7168 + g * 128 + np.arange(128), 8192 + g * 128 + np.arange(128)])
            a.append(_kmajor(wl[:, cols], 8))
        wg.append(np.stack(a))
        wdt.append(_kmajor(wl[:, 9216:9280], 8))
        wgate.append(np.stack([_kmajor(wl[:, 9280:10304], 8), _kmajor(wl[:, 10304:11328], 8)]))
        rp.append(_rpb_tables(inp["rpb"][l]))
    o["w_qkv"] = np.stack(wq)
    o["w_g"] = np.stack(wg)
    o["w_dt"] = np.stack(wdt)
    o["w_gate"] = np.stack(wgate)
    o["rpbt"] = np.stack(rp)
    o["w_br"] = np.stack([_kmajor(inp["w_attn_br"][l], 8) for l in range(L)])
    o["w_ssd"] = np.stack([_kmajor(inp["w_ssd_br"][l], 16) for l in range(L)])
    o["w_o"] = np.stack([_kmajor(inp["w_o"][l], 8) for l in range(L)])
    o["w_ff1"] = np.stack([np.stack([_kmajor(inp["w_ff1"][l][:, fb * 512:(fb + 1) * 512], 8) for fb in range(8)])
                           for l in range(L)])
    o["w_ff2"] = np.stack([np.stack([_kmajor(inp["w_ff2"][l][fb * 512:(fb + 1) * 512, :], 4) for fb in range(8)])
                           for l in range(L)])
    o["bgate"] = np.stack([np.ascontiguousarray(inp["b_gate"][l].reshape(16, 128).T) for l in range(L)])
    o["convw"] = np.stack([np.ascontiguousarray(inp["conv_w"][l].reshape(5, 32, 128).transpose(2, 1, 0).reshape(128, 160))
                           for l in range(L)])
    o["convb_f"] = np.stack([np.ascontiguousarray(inp["conv_b"][l].reshape(32, 128).T) for l in range(L)])
    cb_ = inp["conv_b"].reshape(L, 32, 128)
    o["convb_r"] = np.stack([np.stack([np.concatenate([np.tile(cb_[l, c_], 4) for c_ in (g * 2, g * 2 + 1, 16 + g, 24 + g)])[None, :]
                                       for g in range(8)]) for l in range(L)])
    o["alog"] = np.ascontiguousarray(inp["a_log"].reshape(L, 1, 64))
    o["dtb"] = np.ascontiguousarray(inp["dt_bias"].reshape(L, 1, 64))
    o["dskip"] = np.ascontiguousarray(inp["d_skip"].reshape(L, 1, 32))
    o["normw"] = np.stack([np.ascontiguousarray(inp["ssd_norm_w"][l].reshape(16, 128).T) for l in range(L)])
    return {k: np.ascontiguousarray(v, dtype=f) for k, v in o.items()}


IN_SHAPES = {
    "x": [S_TOK, D], "lnp": [2 + 4 * L, D],
    "c_ident": [128, 128], "c_uf": [128, 128], "c_ub": [128, 128], "c_mf": [128, 128], "c_mb": [128, 128],
    "c_sel": [64, 4096], "c_last": [128, 128], "c_first": [128, 128],
    "w_qkv": [L, 8, 128, 3072], "w_g": [L, 8, 128, 6144], "w_dt": [L, 128, 512], "w_gate": [L, 2, 128, 8192],
    "rpbt": [L, 16, 128, 1920], "w_br": [L, 128, 8192], "w_ssd": [L, 128, 16384], "w_o": [L, 128, 8192],
    "w_ff1": [L, 8, 128, 4096], "w_ff2": [L, 8, 128, 4096], "bgate": [L, 128, 16], "convw": [L, 128, 160],
    "convb_f": [L, 128, 32], "convb_r": [L, 8, 1, 2048], "alog": [L, 1, 64], "dtb": [L, 1, 64], "dskip": [L, 1, 32],
    "normw": [L, 128, 16],
}


class Prog:
    def __init__(self, dbg=()):
        self.nc = bass.Bass("TRN2", target_bir_lowering=False)
        self.S = Sched(self.nc)
        self.dbg = set(dbg)
        self.dbg_out = {}
        nc = self.nc
        self.din = {k: nc.dram_tensor(k, shp, F32, kind="ExternalInput") for k, shp in IN_SHAPES.items()}
        self.y = nc.dram_tensor("y", [S_TOK, D], F32, kind="ExternalOutput")
        self.hres = nc.dram_tensor("hres", [S_TOK, D], F32)
        self.gya = nc.dram_tensor("gya", [8, 128, S_TOK], BF16)
        self.yTd = nc.dram_tensor("yTd", [16, 128, S_TOK], BF16)
        self.r_hres = [Res() for _ in range(NT)]
        self.r_gya = [Res() for _ in range(4)]
        self.r_yTd = [Res() for _ in range(8)]

    def sb(self, st, name, shape, dt, nres=1):
        self.uid = getattr(self, "uid", 0) + 1
        nb = int(np.prod(shape[1:])) * (4 if dt == F32 else 2)
        nb = (nb + 31) // 32 * 32
        self.cur = getattr(self, "cur", 0) + nb
        self.hwm = max(getattr(self, "hwm", 0), self.cur)
        st.callback(lambda: setattr(self, "cur", self.cur - nb))
        return Buf(st.enter_context(self.nc.sbuf_tensor("s%d_%s" % (self.uid, name), shape, dt)), nres)

    def ps(self, st, name, shape, dt, nres=1):
        self.uid = getattr(self, "uid", 0) + 1
        return Buf(st.enter_context(self.nc.psum_tensor("p%d_%s" % (self.uid, name), shape, dt)), nres)

    def mm(self, out, lhsT, rhs, start, stop, reads, writes, **kw):
        self.S.add("pe", lambda e: e.matmul(out, lhsT=lhsT, rhs=rhs, start=start, stop=stop, **kw), reads, writes)

    def tr(self, out, in_, ident, reads, writes):
        self.S.add("pe", lambda e: e.transpose(out=out, in_=in_, identity=ident), reads, writes)

    def act(self, out, in_, func, reads, writes, bias=None, scale=None, accum_out=None, eng="act"):
        kw = {}
        if bias is not None:
            kw["bias"] = bias
        if scale is not None:
            kw["scale"] = scale
        if accum_out is not None:
            kw["accum_out"] = accum_out
        self.S.add(eng, lambda e: e.activation(out=out, in_=in_, func=func, **kw), reads, writes)

    def copy(self, eng, out, in_, reads, writes):
        if eng == "act":
            self.S.add("act", lambda e: e.copy(out=out, in_=in_), reads, writes)
        else:
            self.S.add(eng, lambda e: e.tensor_copy(out=out, in_=in_), reads, writes)

    def tt(self, eng, out, in0, in1, op, reads, writes):
        self.S.add(eng, lambda e: e.tensor_tensor(out=out, in0=in0, in1=in1, op=op), reads, writes)

    def ts(self, eng, out, in0, s1, s2, op0, op1, reads, writes):
        if op1 is None:
            self.S.add(eng, lambda e: e.tensor_scalar(out=out, in0=in0, scalar1=s1, scalar2=None, op0=op0), reads, writes)
        else:
            self.S.add(eng, lambda e: e.tensor_scalar(out=out, in0=in0, scalar1=s1, scalar2=s2, op0=op0, op1=op1),
                       reads, writes)

    def stt(self, eng, out, in0, scalar, in1, op0, op1, reads, writes):
        self.S.add(eng, lambda e: e.scalar_tensor_tensor(out=out, in0=in0, scalar=scalar, in1=in1, op0=op0, op1=op1),
                   reads, writes)

    def memset(self, eng, ap, val, writes):
        self.S.add(eng, lambda e: e.memset(ap, val), (), writes)

    def dma(self, eng, out, in_, reads, writes):
        self.S.add(eng, lambda e: e.dma_start(out=out, in_=in_), reads, writes, dma=True)

    def stager(self, st, n=3):
        bufs = [self.sb(st, "stg%d" % i, [128, 2048], F32) for i in range(n)]
        state = {"i": 0}
        engs = ("act", "dve", "pool")

        def load(dst2d, src2d, dst_res, eng=None):
            i = state["i"]
            state["i"] += 1
            b = bufs[i % n]
            self.dma("sp", b[:], src2d, (), b.r)
            self.copy(eng or engs[i % 2], dst2d, b[:], b.r, dst_res)
        return load

    def dump(self, name, ap_sb, shape, dt, reads):
        t = self.nc.dram_tensor("dbg_" + name, shape, dt, kind="ExternalOutput")
        self.dbg_out["dbg_" + name] = t
        self.dma("sp", t.ap(), ap_sb, reads, ())


def rsqrt_cols(P, out, v, tmp, res):
    P.act(tmp, v, AF.Ln, res, res)
    P.act(out, tmp, AF.Exp, res, res, scale=-0.5)
    P.tt("dve", tmp, v, out, ALU.mult, res, res)
    P.tt("dve", tmp, tmp, out, ALU.mult, res, res)
    P.ts("dve", tmp, tmp, -0.5, 1.5, ALU.mult, ALU.add, res, res)
    P.tt("dve", out, out, tmp, ALU.mult, res, res)


def build(dbg=(), stop_after=None, nlayers=L):
    P = Prog(dbg)
    P.stop = stop_after
    nc, S = P.nc, P.S
    din = P.din
    G = ExitStack()

    identf = P.sb(G, "identf", [128, 128], F32)
    identb = P.sb(G, "identb", [128, 128], BF16)
    onesb = P.sb(G, "onesb", [128, 128], BF16)
    hT = P.sb(G, "hT", [128, 8, S_TOK], BF16, nres=NT)
    P.dma("sp", identf[:], din["c_ident"].ap(), (), identf.r)
    P.copy("dve", identb[:], identf[:], identf.r, identb.r)
    P.memset("dve", onesb[:], 1.0, onesb.r)

    def hT_r(t0, n=1):
        return hT.r[t0:t0 + n]

    def ln_phase_bufs(st, with_xn=True):
        d = {}
        d["gb"] = P.sb(st, "ln_gb", [128, 2, D], F32)
        d["st"] = [P.sb(st, "ln_st%d" % i, [128, 12], F32) for i in range(2)]
        d["mv"] = [P.sb(st, "ln_mv%d" % i, [128, 4], F32) for i in range(2)]
        if with_xn:
            d["xn"] = [P.sb(st, "ln_xn%d" % i, [128, D], F32) for i in range(2)]
        d["hb"] = [P.sb(st, "ln_hb%d" % i, [128, D], BF16) for i in range(2)]
        d["pT"] = [P.ps(st, "ln_pT%d" % i, [128, 8, 128], BF16) for i in range(2)]
        d["n"] = 0
        return d

    def ln_load_params(d, row):
        lnp = din["lnp"].ap()
        P.dma("sp", d["gb"][:, 0, :], lnp[row:row + 1, :].partition_broadcast(128), (), d["gb"].r)
        P.dma("sp", d["gb"][:, 1, :], lnp[row + 1:row + 2, :].partition_broadcast(128), (), d["gb"].r)

    def ln_tile(d, src_ap, src_res, t, out_dram=None, out_res=None, to_hT=True, dst=None):
        i = d["n"] % 2
        d["n"] += 1
        stt_, mv, hb, pT = d["st"][i], d["mv"][i], d["hb"][i], d["pT"][i]
        xn = dst if dst is not None else d["xn"][i]
        ho = xn
        S.add("dve", lambda e: e.bn_stats(stt_[:, 0:6], src_ap[:, 0:512]), src_res, stt_.r)
        S.add("dve", lambda e: e.bn_stats(stt_[:, 6:12], src_ap[:, 512:1024]), src_res, stt_.r)
        S.add("dve", lambda e: e.bn_aggr(mv[:, 0:2], stt_[:, 0:12].rearrange("p (t j) -> p t j", j=3)), stt_.r, mv.r)
        P.ts("dve", mv[:, 1:2], mv[:, 1:2], LN_EPS, None, ALU.add, None, mv.r, mv.r)
        rsqrt_cols(P, mv[:, 2:3], mv[:, 1:2], mv[:, 3:4], mv.r)
        P.stt("dve", xn[:], src_ap, mv[:, 0:1], d["gb"][:, 0, :], ALU.subtract, ALU.mult,
              list(src_res) + mv.r + d["gb"].r, xn.r)
        P.stt("dve", xn[:], xn[:], mv[:, 2:3], d["gb"][:, 1, :], ALU.mult, ALU.add, xn.r + mv.r + d["gb"].r, xn.r)
        if out_dram is not None:
            P.dma("pool", out_dram, ho[:], ho.r, out_res)
        if not to_hT:
            return None
        P.copy("act", hb[:], ho[:], ho.r, hb.r)

        def later():
            for kc in range(8):
                P.tr(pT[:, kc, :], hb[:, kc * 128:(kc + 1) * 128], identb[:], hb.r + identb.r, pT.r)
            P.copy("act", hT[:, :, t * 128:(t + 1) * 128], pT[:], pT.r, hT_r(t))
        return later

    with ExitStack() as st:
        d = ln_phase_bufs(st)
        ln_load_params(d, 0)
        xin = [P.sb(st, "xin%d" % i, [128, D], F32) for i in range(3)]
        x_ap = din["x"].ap()
        hres_ap = P.hres.ap()
        pq = []
        for t in range(NT):
            xb = xin[t % 3]
            P.dma("sp", xb[:], x_ap[t * 128:(t + 1) * 128, :], (), xb.r)
            if len(pq) >= 2:
                pq.pop(0)()
            pq.append(ln_tile(d, xb[:], xb.r, t, out_dram=hres_ap[t * 128:(t + 1) * 128, :], out_res=[P.r_hres[t]]))
        for f_ in pq:
            f_()
        if "hT0" in P.dbg:
            P.dump("hT0", hT[:], [128, 8, S_TOK], BF16, hT.r)
        S.flush()
    if stop_after == "ln0":
        return P

    for l in range(nlayers):
        attention_phase(P, l, hT, identb, onesb)
        if stop_after == "att%d" % l:
            return P
        ssd_phase(P, l, hT, identb, identf, onesb)
        if stop_after in ("ssd%d" % l, "ssdprep%d" % l):
            return P
        merge_phase(P, l, hT, identb, ln_phase_bufs, ln_load_params, ln_tile)
        if stop_after == "merge%d" % l:
            return P
        ffn_phase(P, l, hT, identb, ln_phase_bufs, ln_load_params, ln_tile, last=(l == nlayers - 1))
        if stop_after == "ffn%d" % l:
            return P
    return P


def attention_phase(P, l, hT, identb, onesb):
    S = P.S
    din = P.din
    with ExitStack() as so:
      ao = P.sb(so, "ao", [128, NT, D], BF16, nres=NT)
      with ExitStack() as st:
        wqkv = [P.sb(st, "wqkv%d" % i, [128, 8, 384], BF16) for i in range(2)]
        tab = [P.sb(st, "tab%d" % i, [128, 1920], BF16) for i in range(2)]
        QT = P.sb(st, "QT", [128, S_TOK], BF16)
        KT = P.sb(st, "KT", [128, S_TOK], BF16)
        Vt = P.sb(st, "Vt", [128, NT, 2, 65], BF16)
        PT = [P.sb(st, "PT%d" % i, [128, 512], BF16) for i in range(4)]
        PR = [P.sb(st, "PR%d" % i, [128, 512], BF16) for i in range(4)]
        rc = [P.sb(st, "rc%d" % i, [128, 2], F32) for i in range(2)]
        psA = [P.ps(st, "psA%d" % i, [128, 512], F32) for i in range(2)]
        psS = [P.ps(st, "psS%d" % i, [128, 512], F32) for i in range(4)]
        psO = [P.ps(st, "psO%d" % i, [128, 512], F32) for i in range(2)]
        P.memset("dve", Vt[:, :, :, 64:65], 1.0, Vt.r)
        w_qkv = din["w_qkv"].ap()
        rpbt = din["rpbt"].ap()
        n_s = 0
        n_pt = 0
        n_o = 0
        n_a = 0
        for hp in range(8):
            wb = wqkv[hp % 2]
            P.dma("pool", wb[:].rearrange("p a b -> p (a b)"), w_qkv[l, hp], (), wb.r)
            for which in range(2):
                for n in range(4):
                    ps = psA[n_a % 2]
                    n_a += 1
                    for kc in range(8):
                        P.mm(ps[:], wb[:, kc, which * 128:(which + 1) * 128], hT[:, kc, n * 512:(n + 1) * 512],
                             kc == 0, kc == 7, wb.r + hT.r[n * 4:n * 4 + 4], ps.r)
                    if which == 0:
                        P.act(QT[:, n * 512:(n + 1) * 512], ps[:], AF.Copy, ps.r, QT.r, scale=0.125)
                    else:
                        P.copy("dve", KT[:, n * 512:(n + 1) * 512], ps[:], ps.r, KT.r)
            for t4 in range(4):
                ps = psA[n_a % 2]
                n_a += 1
                for tt in range(4):
                    t = t4 * 4 + tt
                    for kc in range(8):
                        P.mm(ps[:, tt * 128:(tt + 1) * 128], hT[:, kc, t * 128:(t + 1) * 128], wb[:, kc, 256:384],
                             kc == 0, kc == 7, wb.r + hT.r[t:t + 1], ps.r)
                P.copy("dve", Vt[:, t4 * 4:(t4 + 1) * 4, :, 0:64],
                       ps[:].rearrange("p (t h d) -> p t h d", t=4, h=2), ps.r, Vt.r)
            items = []
            for hh in range(2):
                h = 2 * hp + hh
                tb = tab[h % 2]
                P.dma("pool", tb[:], rpbt[l, h], (), tb.r)
                P.act(tb[:], tb[:], AF.Exp, tb.r, tb.r)
                for qb in range(8):
                    R = qb * 4
                    if qb == 0:
                        krs, var = [0, 2, 4, 6], 0
                    elif qb == 7:
                        krs, var = [24, 26, 28, 30], 0
                    else:
                        krs, var = [R - 4, R - 2, R, R + 2, R + 4, R + 6], 1
                    pO = psO[n_o % 2]
                    n_o += 1
                    r_ = rc[n_o % 2]
                    for pi in range(len(krs) // 2):
                        items.append({"hh": hh, "h": h, "tb": tb, "qb": qb, "R": R, "var": var, "krs": krs, "pi": pi,
                                      "pO": pO, "rc": r_})

            def stage_a(it):
                nonlocal n_s
                ps = psS[n_s % 4]
                n_s += 1
                it["ps"] = ps
                po, R = it["hh"] * 64, it["R"]
                for j in range(2):
                    kr0 = it["krs"][2 * it["pi"] + j]
                    P.mm(ps[:, j * 256:(j + 1) * 256], KT[po:po + 64, kr0 * 64:kr0 * 64 + 128],
                         QT[po:po + 64, R * 64:R * 64 + 256], True, True, KT.r + QT.r, ps.r)

            def stage_b(it):
                nonlocal n_pt
                ps, tb = it["ps"], it["tb"]
                pt = PT[n_pt % 4]
                pr = PR[n_pt % 4]
                n_pt += 1
                it["pt"] = pt
                P.act(pr[:], ps[:], AF.Exp, ps.r, pr.r)
                for j in range(2):
                    kr0 = it["krs"][2 * it["pi"] + j]
                    c0 = it["var"] * 960 + (7 - (kr0 - it["R"])) * 64
                    P.tt("dve", pt[:, j * 256:(j + 1) * 256], pr[:, j * 256:(j + 1) * 256], tb[:, c0:c0 + 256], ALU.mult,
                         pr.r + tb.r, pt.r)

            def stage_c(it):
                pt, pO, hh, h, qb = it["pt"], it["pO"], it["hh"], it["h"], it["qb"]
                nk = len(it["krs"])
                for j in range(2):
                    ki = 2 * it["pi"] + j
                    kr0 = it["krs"][ki]
                    for half in range(2):
                        P.mm(pO[:, half * 128:half * 128 + 65], pt[:, j * 256 + half * 128:j * 256 + (half + 1) * 128],
                             Vt[:, kr0 // 2, hh, :], ki == 0 and half == 0, ki == nk - 1, pt.r + Vt.r, pO.r,
                             skip_group_check=True)
                if it["pi"] == nk // 2 - 1:
                    r_ = it["rc"]
                    for half in range(2):
                        t = qb * 2 + half
                        S.add("dve", (lambda e, r_=r_, half=half, pO=pO: e.reciprocal(r_[:, half:half + 1],
                                                                                      pO[:, half * 128 + 64:half * 128 + 65])),
                              pO.r, r_.r)
                        P.ts("dve", ao[:, t, h * 64:(h + 1) * 64], pO[:, half * 128:half * 128 + 64], r_[:, half:half + 1], None,
                             ALU.mult, None, pO.r + r_.r, ao.r[t:t + 1])

            LAG = 3
            for k in range(len(items) + LAG):
                if k < len(items):
                    stage_a(items[k])
                    stage_b(items[k])
                if k >= LAG:
                    stage_c(items[k - LAG])
        if "ao%d" % l in P.dbg:
            P.dump("ao%d" % l, ao[:], [128, NT, D], BF16, ao.r)
        S.flush()
      if True:
        with ExitStack() as st2:
            aoT = P.sb(st2, "aoT", [128, 8, S_TOK], BF16, nres=NT)
            wbr = P.sb(st2, "wbr", [128, 8, D], BF16)
            wga = P.sb(st2, "wga", [128, 8, D], BF16)
            bg = P.sb(st2, "bg", [128, 16], F32)
            gs = [P.sb(st2, "gs%d" % i, [128, 512], F32) for i in range(2)]
            stage = [P.sb(st2, "gyast%d" % i, [128, 8, 512], BF16) for i in range(2)]
            pT = [P.ps(st2, "a2pT%d" % i, [128, 8, 128], BF16) for i in range(2)]
            psY = [P.ps(st2, "psY%d" % i, [128, 512], F32) for i in range(2)]
            psG = [P.ps(st2, "psG%d" % i, [128, 512], F32) for i in range(2)]
            ld = P.stager(st2)
            for q in range(4):
                ld(wbr[:, q * 2:(q + 1) * 2, :].rearrange("p a b -> p (a b)"), din["w_br"].ap()[l, :, q * 2048:(q + 1) * 2048], wbr.r)
                ld(wga[:, q * 2:(q + 1) * 2, :].rearrange("p a b -> p (a b)"), din["w_gate"].ap()[l, 0, :, q * 2048:(q + 1) * 2048],
                   wga.r)
            P.dma("sp", bg[:], din["bgate"].ap()[l], (), bg.r)
            for t in range(NT):
                p = pT[t % 2]
                for kc in range(8):
                    P.tr(p[:, kc, :], ao[:, t, kc * 128:(kc + 1) * 128], identb[:], ao.r[t:t + 1] + identb.r, p.r)
                P.copy("act" if t % 2 else "dve", aoT[:, :, t * 128:(t + 1) * 128], p[:], p.r, aoT.r[t:t + 1])
            gya_ap = P.gya.ap()
            k = 0
            for n in range(4):
                sg = stage[n % 2]
                for c in range(8):
                    py, pg, g_ = psY[k % 2], psG[k % 2], gs[k % 2]
                    k += 1
                    for kc in range(8):
                        P.mm(py[:], wbr[:, kc, c * 128:(c + 1) * 128], aoT[:, kc, n * 512:(n + 1) * 512],
                             kc == 0, kc == 7, wbr.r + aoT.r[n * 4:n * 4 + 4], py.r)
                    for kc in range(8):
                        P.mm(pg[:], wga[:, kc, c * 128:(c + 1) * 128], hT[:, kc, n * 512:(n + 1) * 512],
                             kc == 0, kc == 7, wga.r + hT.r[n * 4:n * 4 + 4], pg.r)
                    P.act(g_[:], pg[:], AF.Sigmoid, pg.r + bg.r, g_.r, bias=bg[:, c:c + 1])
                    P.tt("dve", sg[:, c, :], py[:], g_[:], ALU.mult, py.r + g_.r, sg.r)
                P.dma("sp", gya_ap[:, :, n * 512:(n + 1) * 512].rearrange("c p t -> p c t"), sg[:], sg.r, [P.r_gya[n]])
            S.flush()


def ssd_phase(P, l, hT, identb, identf, onesb):
    S = P.S
    din = P.din
    f32b = lambda st, name, shape, nres=1: P.sb(st, name, shape, F32, nres)
    with ExitStack() as so:
        uf = f32b(so, "uf", [128, 128])
        ub = f32b(so, "ub", [128, 128])
        lastm = f32b(so, "lastm", [128, 128])
        firstm = f32b(so, "firstm", [128, 128])
        sel = P.sb(so, "sel", [64, 512], BF16)
        mask = [P.sb(so, "mask%d" % i, [128, 4, 128], BF16) for i in range(2)]
        bias_tok = f32b(so, "bias_tok", [128, NT, 64])
        e_tok = f32b(so, "e_tok", [128, NT, 64])
        dtdec = f32b(so, "dtdec", [128, NT, 64])
        decay_bc = f32b(so, "decay_bc", [128, NT, 64])
        acsHL = [P.sb(so, "acsHL%d" % i, [64, S_TOK], BF16) for i in range(2)]
        dskip_bc = f32b(so, "dskip_bc", [128, 32])
        normw16 = f32b(so, "normw16", [128, 16])
        convw = f32b(so, "convw", [128, 160])
        convb_f = f32b(so, "convb_f", [128, 32])
        convb_r = P.sb(so, "convb_r", [1, 2048], BF16)
        P.dma("sp", uf[:], din["c_uf"].ap(), (), uf.r)
        P.dma("sp", ub[:], din["c_ub"].ap(), (), ub.r)
        P.dma("sp", lastm[:], din["c_last"].ap(), (), lastm.r)
        P.dma("sp", firstm[:], din["c_first"].ap(), (), firstm.r)
        P.dma("sp", dskip_bc[:], din["dskip"].ap()[l].partition_broadcast(128), (), dskip_bc.r)
        P.dma("sp", normw16[:], din["normw"].ap()[l], (), normw16.r)
        P.dma("sp", convw[:], din["convw"].ap()[l], (), convw.r)
        P.dma("sp", convb_f[:], din["convb_f"].ap()[l], (), convb_f.r)
        P.ts("dve", normw16[:], normw16[:], 16.0, None, ALU.mult, None, normw16.r, normw16.r)
        with ExitStack() as st:
            mtmp = f32b(st, "mtmp", [128, 128])
            wdt = P.sb(st, "wdt", [128, 8, 64], BF16)
            dtb_bc = f32b(st, "dtb_bc", [128, 64])
            a_bc = f32b(st, "a_bc", [128, 64])
            x_ = f32b(st, "x_", [128, NT, 64])
            dt_ = f32b(st, "dt_", [128, NT, 64])
            adt = f32b(st, "adt", [128, NT, 64])
            acs = f32b(st, "acs", [128, NT, 64])
            last_bc = f32b(st, "last_bc", [128, NT, 64])
            tmp = f32b(st, "tmp", [128, NT, 64])
            psD = [P.ps(st, "psD%d" % i, [128, 8, 64], F32) for i in range(2)]
            psC = [P.ps(st, "psC%d" % i, [128, 8, 64], F32) for i in range(2)]
            psT = [P.ps(st, "psT%d" % i, [64, 512], F32) for i in range(2)]
            adtd = f32b(st, "adtd", [128, NT, 2, 64])
            psL = [P.ps(st, "psL%d" % i, [128, NT, 32], F32) for i in range(2)]
            for i, nm in enumerate(("c_mf", "c_mb")):
                P.dma("sp", mtmp[:], din[nm].ap(), (), mtmp.r)
                for cc in range(4):
                    P.copy("dve", mask[i][:, cc, :], mtmp[:], mtmp.r, mask[i].r)
            P.dma("pool", wdt[:].rearrange("p a b -> p (a b)"), din["w_dt"].ap()[l], (), wdt.r)
            P.dma("sp", dtb_bc[:], din["dtb"].ap()[l].partition_broadcast(128), (), dtb_bc.r)
            P.dma("sp", a_bc[:], din["alog"].ap()[l].partition_broadcast(128), (), a_bc.r)
            P.act(a_bc[:], a_bc[:], AF.Exp, a_bc.r, a_bc.r)
            P.ts("dve", a_bc[:], a_bc[:], -1.0, None, ALU.mult, None, a_bc.r, a_bc.r)
            for half in range(2):
                for tt in range(8):
                    t = half * 8 + tt
                    for kc in range(8):
                        P.mm(psD[half][:, tt, :], hT[:, kc, t * 128:(t + 1) * 128], wdt[:, kc, :], kc == 0, kc == 7,
                             hT.r[t:t + 1] + wdt.r, psD[half].r)
                P.tt("dve", x_[:, half * 8:(half + 1) * 8, :], psD[half][:],
                     dtb_bc[:].unsqueeze(1).broadcast_to([128, 8, 64]), ALU.add, psD[half].r + dtb_bc.r, x_.r)
            P.act(tmp[:], x_[:], AF.Exp, x_.r, tmp.r)
            P.act(dt_[:], tmp[:], AF.Ln, tmp.r, dt_.r, bias=1.0)
            P.tt("dve", adt[:], dt_[:], a_bc[:].unsqueeze(1).broadcast_to([128, NT, 64]), ALU.mult, dt_.r + a_bc.r, adt.r)
            for c in range(NT):
                pc = psC[c // 8]
                P.mm(pc[:, c % 8, 0:32], uf[:], adt[:, c, 0:32], True, True, uf.r + adt.r, pc.r)
                P.mm(pc[:, c % 8, 32:64], ub[:], adt[:, c, 32:64], True, True, ub.r + adt.r, pc.r)
            for half in range(2):
                P.copy("dve", acs[:, half * 8:(half + 1) * 8, :], psC[half][:], psC[half].r, acs.r)
            k = 0
            for d_ in range(2):
                for hf_ in range(2):
                    P.copy("dve", adtd[:, :, d_, hf_ * 32:(hf_ + 1) * 32], adt[:, :, d_ * 32:(d_ + 1) * 32], adt.r, adtd.r)
            for d_ in range(2):
                um = uf if d_ == 0 else ub
                for cb in range(4):
                    pt = psT[k % 2]
                    k += 1
                    for cc in range(4):
                        c = cb * 4 + cc
                        P.mm(pt[:, cc * 128:(cc + 1) * 128], adtd[:, c, d_, :], um[:], True, True,
                             adtd.r + um.r, pt.r)
                    blk_ = slice(cb * 512, (cb + 1) * 512)
                    P.copy("act", acsHL[d_][:, blk_], pt[:], pt.r, acsHL[d_].r)
                    P.tt("dve", acsHL[d_][32:64, blk_], pt[32:64, :], acsHL[d_][32:64, blk_], ALU.subtract,
                         pt.r + acsHL[d_].r, acsHL[d_].r)
            P.mm(psL[0][:], lastm[:], acs[:, :, 0:32], True, True, lastm.r + acs.r, psL[0].r)
            P.mm(psL[1][:], firstm[:], acs[:, :, 32:64], True, True, firstm.r + acs.r, psL[1].r)
            for d_ in range(2):
                P.copy("dve", last_bc[:, :, d_ * 32:(d_ + 1) * 32], psL[d_][:], psL[d_].r, last_bc.r)
            P.act(decay_bc[:], last_bc[:], AF.Exp, last_bc.r, decay_bc.r)
            P.tt("dve", tmp[:], last_bc[:], acs[:], ALU.subtract, last_bc.r + acs.r, tmp.r)
            P.act(tmp[:], tmp[:], AF.Exp, tmp.r, tmp.r)
            P.tt("dve", dtdec[:], dt_[:], tmp[:], ALU.mult, dt_.r + tmp.r, dtdec.r)
            P.act(e_tok[:], acs[:], AF.Exp, acs.r, e_tok.r)
            P.act(tmp[:], dt_[:], AF.Ln, dt_.r, tmp.r)
            P.tt("dve", bias_tok[:], tmp[:], acs[:], ALU.subtract, tmp.r + acs.r, bias_tok.r)
            if "dt%d" % l in P.dbg:
                P.dump("dt%d" % l, dt_[:], [128, NT, 64], F32, dt_.r)
                P.dump("acs%d" % l, acs[:], [128, NT, 64], F32, acs.r)
            S.flush()
        if P.stop == "ssdprep%d" % l:
            return
        with ExitStack() as st:
            Wgs = [P.sb(st, "Wg%d" % i, [128, 8, 768], BF16) for i in range(2)]
            uT = [P.sb(st, "uT%d" % i, [128, 2052], BF16) for i in range(2)]
            zs = P.sb(st, "zs", [128, NT, 256], BF16, nres=NT)
            xs_bf = P.sb(st, "xs_bf", [128, NT, 256], BF16)
            xdec = [P.sb(st, "xdec%d" % i, [128, 2, 256], BF16) for i in range(2)]
            Bt = P.sb(st, "Bt", [128, NT, 128], BF16)
            BT = P.sb(st, "BT", [128, S_TOK], BF16)
            CT = P.sb(st, "CT", [128, S_TOK], BF16)
            CBT = P.sb(st, "CBT", [128, NT, 128], BF16)
            prevb = [P.sb(st, "prevb%d" % i, [128, NT, 256], BF16) for i in range(2)]
            Sst = [[f32b(st, "Sst%d_%d" % (i, j), [128, 256]) for j in range(2)] for i in range(2)]
            E = [P.sb(st, "E%d" % i, [128, 512], BF16) for i in range(2)]
            MT = [[P.sb(st, "MT%d_%d" % (i, d_), [128, 512], BF16) for d_ in range(2)] for i in range(2)]
            dgs = [P.sb(st, "dg%d" % i, [128, 4, 5, 128], BF16) for i in range(2)]
            Yb = [f32b(st, "Yb%d" % i, [128, 256]) for i in range(2)]
            Yg = [f32b(st, "Yg%d" % i, [128, 4, 256]) for i in range(2)]
            ss = [f32b(st, "ss%d" % i, [128, 12]) for i in range(2)]
            junk = f32b(st, "junk", [128, 256])
            yTst = [P.sb(st, "yTst%d" % i, [128, 2, 512], BF16) for i in range(2)]
            yoS = [f32b(st, "yoS%d" % i, [128, 512]) for i in range(2)]
            Yn = [P.sb(st, "Yn%d" % i, [128, 4, 256], BF16) for i in range(2)]
            pend = [None]
            bk = [P.ps(st, "bk%d" % i, [128, 512], F32) for i in range(7)]
            psTr = [P.ps(st, "psTr", [128, 2, 4, 128], BF16)]
            for u in uT:
                P.memset("dve", u[:, 0:2], 0.0, u.r)
                P.memset("dve", u[:, 2050:2052], 0.0, u.r)
            yTd = P.yTd.ap()
            w_g = din["w_g"].ap()
            cnt = {"u": 0, "x": 0, "f": 0, "z": 0, "seg": 0, "e": 0, "blk": 0, "yn": 0}
            Dsk = P.sb(st, "Dsk", [128, 4, 128], BF16)
            T1 = [f32b(st, "T1_%d" % i, [128, 512]) for i in range(2)]
            lagq = []

            def flush_lag():
                q = list(lagq)
                del lagq[:]
                for f_ in q:
                    f_()

            ngroups = 1 if "ssd_g1" in P.dbg else 8
            P.dma("pool", Wgs[0][:].rearrange("p a b -> p (a b)"), w_g[l, 0], (), Wgs[0].r)
            for g in range(ngroups):
                Wg = Wgs[g % 2]
                if g + 1 < ngroups:
                    P.dma("pool", Wgs[(g + 1) % 2][:].rearrange("p a b -> p (a b)"), w_g[l, g + 1], (), Wgs[(g + 1) % 2].r)
                P.dma("pool", sel[:], din["c_sel"].ap()[:, g * 512:(g + 1) * 512], (), sel.r)
                P.dma("pool", convb_r[:], din["convb_r"].ap()[l, g], (), convb_r.r)
                chs = [g * 2, g * 2 + 1, 16 + g, 24 + g]

                dg = dgs[g % 2]

                def build_dg(gg, part=None):
                    cs = [gg * 2, gg * 2 + 1, 16 + gg, 24 + gg]
                    dd = dgs[gg % 2]
                    for k in range(4):
                        for j in range(5):
                            if part is not None and (k * 5 + j) // 2 != part:
                                continue
                            col = cs[k] * 5 + j
                            P.act(dd[:, k, j, :], identf[:], AF.Copy, identf.r + convw.r, dd.r, scale=convw[:, col:col + 1])

                if g == 0:
                    build_dg(0)
                for r in range(4):
                    P.act(Dsk[:, r, :], identf[:], AF.Copy, identf.r + dskip_bc.r, Dsk.r,
                          scale=dskip_bc[:, g * 4 + r:g * 4 + r + 1])

                def proj_u(k):
                    u = uT[k % 2]
                    c0 = 256 + k * 128
                    for n in range(4):
                        b = bk[3 + cnt["u"] % 2]
                        cnt["u"] += 1
                        for kc in range(8):
                            P.mm(b[:], Wg[:, kc, c0:c0 + 128], hT[:, kc, n * 512:(n + 1) * 512], kc == 0, kc == 7,
                                 Wg.r + hT.r[n * 4:n * 4 + 4], b.r)
                        P.copy("dve" if (n % 2 and k < 3) else "act", u[:, 2 + n * 512:2 + (n + 1) * 512], b[:], b.r, u.r)

                def conv_tok(k, t4):
                    u = uT[k % 2]
                    b = bk[5 + cnt["x"] % 2]
                    cnt["x"] += 1
                    P.mm(b[:], onesb[0:1, 0:128], convb_r[0:1, k * 512:(k + 1) * 512], True, False,
                         onesb.r + convb_r.r, b.r)
                    for tt in range(4):
                        t = t4 * 4 + tt
                        o = b[:, tt * 128:(tt + 1) * 128]
                        for j in range(5):
                            P.mm(o, u[:, t * 128 + j:t * 128 + j + 128], dg[:, k, j, :], False, tt == 3 and j == 4,
                                 u.r + dg.r, b.r)
                    if k < 2:
                        P.act(xs_bf[:, t4 * 4:(t4 + 1) * 4, k * 128:(k + 1) * 128],
                              b[:].rearrange("p (t c) -> p t c", t=4), AF.Silu, b.r, xs_bf.r)
                    else:
                        P.act(Bt[:, t4 * 4:(t4 + 1) * 4, :], b[:].rearrange("p (t c) -> p t c", t=4), AF.Silu, b.r, Bt.r)

                def conv_feat(k, n):
                    u = uT[k % 2]
                    dst = BT if k == 2 else CT
                    b = (bk[1], bk[2])[cnt["f"] % 2] if k == 2 else bk[4]
                    cnt["f"] += 1
                    for j in range(5):
                        P.mm(b[:], dg[:, k, j, :], u[:, n * 512 + j:n * 512 + j + 512], j == 0, j == 4, u.r + dg.r, b.r)
                    P.act(dst[:, n * 512:(n + 1) * 512], b[:], AF.Silu, b.r + convb_f.r, dst.r,
                          bias=convb_f[:, chs[k]:chs[k] + 1])

                def z_tile(t):
                    b = bk[cnt["z"] % 2]
                    cnt["z"] += 1
                    for kc in range(8):
                        P.mm(b[:, 0:256], hT[:, kc, t * 128:(t + 1) * 128], Wg[:, kc, 0:256],
                             kc == 0, kc == 7, hT.r[t:t + 1] + Wg.r, b.r)
                    P.act(zs[:, t, :], b[:, 0:256], AF.Silu, b.r, zs.r[t:t + 1])

                def cbt_block(cb):
                    b = bk[4]
                    for cc in range(4):
                        c = cb * 4 + cc
                        P.mm(b[:, cc * 128:(cc + 1) * 128], BT[:, c * 128:(c + 1) * 128], CT[:, c * 128:(c + 1) * 128],
                             True, True, BT.r + CT.r, b.r)
                    P.copy("act", CBT[:, cb * 4:(cb + 1) * 4, :], b[:].rearrange("p (t c) -> p t c", t=4), b.r, CBT.r)

                def scan_step(i):
                    xd = xdec[i % 2]
                    for d_ in range(2):
                        c = i if d_ == 0 else NT - 1 - i
                        hb = d_ * 32 + g * 4
                        so_, sn_ = Sst[d_][i % 2], Sst[d_][(i + 1) % 2]
                        P.copy("act", prevb[d_][:, c, :], so_[:], so_.r, prevb[d_].r)
                        if i == NT - 1:
                            continue
                        P.tt("pool", xd[:, d_, :].rearrange("p (r d) -> p r d", r=4),
                             xs_bf[:, c, :].rearrange("p (r d) -> p r d", r=4),
                             dtdec[:, c, hb:hb + 4].unsqueeze(2).broadcast_to([128, 4, 64]), ALU.mult,
                             xs_bf.r + dtdec.r, xd.r)
                        b = bk[5 + i % 2] if d_ == 0 else bk[2 + i % 2]
                        P.mm(b[:, 0:256], Bt[:, c, :], xd[:, d_, :], True, True, Bt.r + xd.r, b.r)
                        P.tt("dve", sn_[:].rearrange("p (r d) -> p r d", r=4),
                             so_[:].rearrange("p (r d) -> p r d", r=4),
                             decay_bc[:, c, hb:hb + 4].unsqueeze(2).broadcast_to([128, 4, 64]), ALU.mult,
                             so_.r + decay_bc.r, sn_.r)
                        P.tt("dve", sn_[:], sn_[:], b[:, 0:256], ALU.add, sn_.r + b.r, sn_.r)

                for k in range(3):
                    proj_u(k)
                    if k == 0 and pend[0] is not None:
                        pend[0]()
                        pend[0] = None
                    for t4 in range(4):
                        conv_tok(k, t4)
                    if k == 2:
                        for n in range(4):
                            conv_feat(2, n)
                for d_ in range(2):
                    P.memset("dve", Sst[d_][0][:], 0.0, Sst[d_][0].r)
                extra = [lambda: proj_u(3)] + [(lambda n=n: conv_feat(3, n)) for n in range(4)]
                for i in range(NT):
                    scan_step(i)
                    z_tile(i)
                    if i < len(extra):
                        extra[i]()
                    if i >= 12:
                        cbt_block(i - 12)
                    if 4 <= i < 14 and g + 1 < ngroups:
                        build_dg(g + 1, part=i - 4)

                def seg_head(cb, r):
                    mts = MT[r % 2]
                    for d_ in range(2):
                        head = d_ * 32 + g * 4 + r
                        sb_ = bk[3 + cnt["seg"] % 2]
                        cnt["seg"] += 1
                        P.mm(sb_[:], sel[0:64, r * 128:(r + 1) * 128],
                             acsHL[d_][0:64, cb * 512:(cb + 1) * 512], True, False, sel.r + acsHL[d_].r, sb_.r)
                        P.mm(sb_[:], identb[:], mask[d_][:].rearrange("p a b -> p (a b)"), False, True,
                             identb.r + mask[d_].r, sb_.r)
                        e_ = E[cnt["e"] % 2]
                        cnt["e"] += 1
                        for cc in range(4):
                            c = cb * 4 + cc
                            P.act(e_[:, cc * 128:(cc + 1) * 128], sb_[:, cc * 128:(cc + 1) * 128], AF.Exp,
                                  sb_.r + bias_tok.r, e_.r, bias=bias_tok[:, c, head:head + 1])
                        P.tt("pool", mts[d_][:], e_[:],
                             CBT[:, cb * 4:(cb + 1) * 4, :].rearrange("p a b -> p (a b)"), ALU.mult,
                             e_.r + CBT.r, mts[d_].r)
                    return mts

                def ydiag_head(cb, r, mts, ydb):
                    for cc in range(4):
                        c = cb * 4 + cc
                        b = ydb[cc // 2]
                        hf = cc % 2
                        o = b[:, hf * 256 + r * 64:hf * 256 + (r + 1) * 64]
                        rhs = xs_bf[:, c, r * 64:(r + 1) * 64]
                        P.mm(o, mts[0][:, cc * 128:(cc + 1) * 128], rhs, True, False, mts[0].r + xs_bf.r, b.r)
                        P.mm(o, mts[1][:, cc * 128:(cc + 1) * 128], rhs, False, False, mts[1].r + xs_bf.r, b.r)
                        P.mm(o, Dsk[:, r, :], rhs, False, True, Dsk.r + xs_bf.r, b.r)

                def combine_chunk(blk, cc):
                    cb, bi, ydb = blk
                    flush_lag()
                    yg, ssb = Yg[bi], ss[bi]
                    if cc == 0:
                        P.memset("dve", ssb[:, 0:4], 0.0, ssb.r)
                    c = cb * 4 + cc
                    hf = cc % 2
                    byd = ydb[cc // 2]
                    byo = bk[1]
                    t1 = T1[cc % 2]
                    for d_ in range(2):
                        P.mm(byo[:, d_ * 256:(d_ + 1) * 256], CT[:, c * 128:(c + 1) * 128], prevb[d_][:, c, :], True, True,
                             CT.r + prevb[d_].r, byo.r)
                    e_bc = e_tok[:, c, :].rearrange("p (d h) -> p d h", d=2)[:, :, g * 4:(g + 1) * 4]
                    P.tt("dve", t1[:].rearrange("p (d r x) -> p d r x", d=2, r=4),
                         byo[:].rearrange("p (d r x) -> p d r x", d=2, r=4),
                         e_bc.unsqueeze(3).broadcast_to([128, 2, 4, 64]), ALU.mult, byo.r + e_tok.r, t1.r)
                    Y = Yb[cc % 2]
                    P.tt("dve", Y[:], t1[:, 0:256], t1[:, 256:512], ALU.add, t1.r, Y.r)
                    P.tt("dve", Y[:], Y[:], byd[:, hf * 256:(hf + 1) * 256], ALU.add, Y.r + byd.r, Y.r)
                    P.tt("dve", yg[:, cc, :], Y[:], zs[:, c, :], ALU.mult, Y.r + zs.r[c:c + 1], yg.r)
                    lagq.append(lambda: P.act(junk[:], yg[:, cc, :], AF.Square, yg.r, junk.r + ssb.r,
                                              accum_out=ssb[:, cc:cc + 1]))

                def combine_finish(blk):
                    cb, bi, ydb = blk
                    flush_lag()
                    yg, ssb, yst, ynb = Yg[bi], ss[bi], yTst[bi], Yn[bi]
                    P.ts("dve", ssb[:, 0:4], ssb[:, 0:4], 256.0 * RMS_EPS, None, ALU.add, None, ssb.r, ssb.r)
                    rsqrt_cols(P, ssb[:, 4:8], ssb[:, 0:4], ssb[:, 8:12], ssb.r)
                    for cc in range(4):
                        P.ts("dve", ynb[:, cc, :], yg[:, cc, :], ssb[:, 4 + cc:5 + cc], None, ALU.mult, None, yg.r + ssb.r, ynb.r)
                    if pend[0] is not None:
                        pend[0]()

                    def later(ynb=ynb, yst=yst, g=g, cb=cb):
                        ptr = psTr[0]
                        for cc in range(4):
                            for k in range(2):
                                P.tr(ptr[:, k, cc, :], ynb[:, cc, k * 128:(k + 1) * 128], identb[:], ynb.r + identb.r, ptr.r)
                        for k in range(2):
                            P.ts("dve", yst[:, k, :], ptr[:, k, :, :].rearrange("p a b -> p (a b)"),
                                 normw16[:, g * 2 + k:g * 2 + k + 1], None, ALU.mult, None, ptr.r + normw16.r, yst.r)
                        P.dma("sp", yTd[g * 2:g * 2 + 2, :, cb * 512:(cb + 1) * 512].rearrange("k p t -> p k t"), yst[:],
                              yst.r, [P.r_yTd[g]])
                    pend[0] = later

                prev_blk = None
                for cb in range(4):
                    bi = cnt["blk"] % 2
                    cnt["blk"] += 1
                    ydb = (bk[5], bk[6]) if bi == 0 else (bk[0], bk[2])
                    mts_prev = None
                    for r in range(4):
                        mts = seg_head(cb, r)
                        if r > 0:
                            ydiag_head(cb, r - 1, mts_prev, ydb)
                        mts_prev = mts
                        if prev_blk is not None:
                            combine_chunk(prev_blk, r)
                    ydiag_head(cb, 3, mts_prev, ydb)
                    if prev_blk is not None:
                        combine_finish(prev_blk)
                    prev_blk = (cb, bi, ydb)
                for cc in range(4):
                    combine_chunk(prev_blk, cc)
                combine_finish(prev_blk)
            if pend[0] is not None:
                pend[0]()
                pend[0] = None
            if "yT%d" % l in P.dbg:
                t_ = P.nc.dram_tensor("dbg_yT%d" % l, [16, 128, S_TOK], BF16, kind="ExternalOutput")
                for kk in range(16):
                    P.dma("sp", t_.ap()[kk], yTd[kk], [P.r_yTd[kk // 2]], ())
            S.flush()


def merge_phase(P, l, hT, identb, ln_phase_bufs, ln_load_params, ln_tile):
    S = P.S
    din = P.din
    with ExitStack() as st:
        d = ln_phase_bufs(st)
        ln_load_params(d, 2 + 4 * l)
        wssd = P.sb(st, "wssd", [128, 16, D], BF16)
        wo = P.sb(st, "wo", [128, 8, D], BF16)
        wgs = P.sb(st, "wgs", [128, 8, D], BF16)
        bg = P.sb(st, "bg", [128, 16], F32)
        yTbs = [P.sb(st, "yTb%d" % i, [128, 16, 512], BF16) for i in range(2)]
        gyab = P.sb(st, "gyab", [128, 8, 512], BF16)
        mT = P.sb(st, "mT", [128, 8, 512], BF16)
        gsg = [P.sb(st, "gsg%d" % i, [128, 512], F32) for i in range(2)]
        tmpf = [P.sb(st, "tmpf%d" % i, [128, 512], F32) for i in range(2)]
        hin = [P.sb(st, "hin%d" % i, [128, D], F32) for i in range(2)]
        psY = [P.ps(st, "mpsY%d" % i, [128, 512], F32) for i in range(2)]
        psG = [P.ps(st, "mpsG%d" % i, [128, 512], F32) for i in range(2)]
        psO = [P.ps(st, "mpsO%d" % i, [128, 512], F32) for i in range(2)]
        w_ssd = din["w_ssd"].ap()
        ld = P.stager(st)
        for q in range(4):
            ld(wgs[:, q * 2:(q + 1) * 2, :].rearrange("p a b -> p (a b)"), din["w_gate"].ap()[l, 1, :, q * 2048:(q + 1) * 2048], wgs.r)
        for q in range(8):
            ld(wssd[:, q * 2:(q + 1) * 2, :].rearrange("p a b -> p (a b)"), w_ssd[l, :, q * 2048:(q + 1) * 2048], wssd.r)
        for q in range(4):
            ld(wo[:, q * 2:(q + 1) * 2, :].rearrange("p a b -> p (a b)"), din["w_o"].ap()[l, :, q * 2048:(q + 1) * 2048], wo.r)
        P.dma("sp", bg[:], din["bgate"].ap()[l], (), bg.r)
        yTd = P.yTd.ap()
        gya = P.gya.ap()
        hres = P.hres.ap()
        k = 0
        pq = []
        P.dma("sp", yTbs[0][:], yTd[:, :, 0:512].rearrange("k p t -> p k t"), P.r_yTd, yTbs[0].r)
        for n in range(4):
            yTb = yTbs[n % 2]
            P.dma("sp", gyab[:], gya[:, :, n * 512:(n + 1) * 512].rearrange("k p t -> p k t"), [P.r_gya[n]], gyab.r)
            if n + 1 < 4:
                P.dma("sp", yTbs[(n + 1) % 2][:], yTd[:, :, (n + 1) * 512:(n + 2) * 512].rearrange("k p t -> p k t"), P.r_yTd,
                      yTbs[(n + 1) % 2].r)
            for c in range(8):
                py, pg, g_, tf = psY[k % 2], psG[k % 2], gsg[k % 2], tmpf[k % 2]
                k += 1
                for kk in range(16):
                    P.mm(py[:], wssd[:, kk, c * 128:(c + 1) * 128], yTb[:, kk, :], kk == 0, kk == 15, wssd.r + yTb.r, py.r)
                for kc in range(8):
                    P.mm(pg[:], wgs[:, kc, c * 128:(c + 1) * 128], hT[:, kc, n * 512:(n + 1) * 512], kc == 0, kc == 7,
                         wgs.r + hT.r[n * 4:n * 4 + 4], pg.r)
                P.act(g_[:], pg[:], AF.Sigmoid, pg.r + bg.r, g_.r, bias=bg[:, 8 + c:9 + c])
                P.tt("dve", tf[:], py[:], g_[:], ALU.mult, py.r + g_.r, tf.r)
                P.tt("pool", mT[:, c, :], tf[:], gyab[:, c, :], ALU.add, tf.r + gyab.r, mT.r)
            for tt in range(4):
                t = n * 4 + tt
                hi = hin[t % 2]
                P.dma("sp", hi[:], hres[t * 128:(t + 1) * 128, :], [P.r_hres[t]], hi.r)
                for half in range(2):
                    po = (psO if t % 2 == 0 else psY)[half]
                    for kc in range(8):
                        P.mm(po[:], mT[:, kc, tt * 128:(tt + 1) * 128], wo[:, kc, half * 512:(half + 1) * 512], kc == 0, kc == 7,
                             mT.r + wo.r, po.r)
                    P.stt("dve", hi[:, half * 512:(half + 1) * 512], hi[:, half * 512:(half + 1) * 512], DN_ALPHA, po[:],
                          ALU.mult, ALU.add, hi.r + po.r, hi.r)
                if len(pq) >= 2:
                    pq.pop(0)()
                pq.append(ln_tile(d, hi[:], hi.r, t, out_dram=hres[t * 128:(t + 1) * 128, :], out_res=[P.r_hres[t]]))
        for f_ in pq:
            f_()
        if "h1_%d" % l in P.dbg:
            t_ = P.nc.dram_tensor("dbg_h1_%d" % l, [S_TOK, D], F32, kind="ExternalOutput")
            for t in range(NT):
                P.dma("sp", t_.ap()[t * 128:(t + 1) * 128, :], hres[t * 128:(t + 1) * 128, :], [P.r_hres[t]], ())
        S.flush()


class _View:
    def __init__(self, ap, r):
        self.ap = ap
        self.r = r

    def __getitem__(self, k):
        return self.ap


def ffn_phase(P, l, hT, identb, ln_phase_bufs, ln_load_params, ln_tile, last):
    S = P.S
    din = P.din
    with ExitStack() as st:
        d = ln_phase_bufs(st, with_xn=False)
        ln_load_params(d, 4 + 4 * l)
        acc = P.sb(st, "acc", [128, NT, D], F32, nres=NT)
        u = [P.sb(st, "ffu%d" % i, [128, 4, S_TOK], BF16) for i in range(2)]
        w1 = [P.sb(st, "w1_%d" % i, [128, 8, 512], BF16) for i in range(2)]
        w2 = [P.sb(st, "w2_%d" % i, [128, 4, D], BF16) for i in range(2)]
        rl = [P.sb(st, "rl%d" % i, [128, 512], F32) for i in range(2)]
        psH = [P.ps(st, "psH%d" % i, [128, 512], F32) for i in range(2)]
        psO = [P.ps(st, "fpsO%d" % i, [128, 512], F32) for i in range(4)]
        ld = P.stager(st, n=2)
        w_ff1 = din["w_ff1"].ap()
        w_ff2 = din["w_ff2"].ap()
        hres = P.hres.ap()
        y_ap = P.y.ap()
        for t in range(NT):
            P.dma("sp", acc[:, t, :], hres[t * 128:(t + 1) * 128, :], [P.r_hres[t]], acc.r[t:t + 1])
            P.act(acc[:, t, :], acc[:, t, :], AF.Copy, acc.r[t:t + 1], acc.r[t:t + 1], scale=DN_ALPHA)
        cnt = {"h": 0, "o": 0}

        def load_w(fb):
            for hh in range(2):
                ld(w1[fb % 2][:, hh * 4:(hh + 1) * 4, :].rearrange("p a b -> p (a b)"),
                   w_ff1[l, fb, :, hh * 2048:(hh + 1) * 2048], w1[fb % 2].r)
            for hh in range(2):
                ld(w2[fb % 2][:, hh * 2:(hh + 1) * 2, :].rearrange("p a b -> p (a b)"),
                   w_ff2[l, fb, :, hh * 2048:(hh + 1) * 2048], w2[fb % 2].r, eng="pool")

        def stage1(fb):
            wb, ub = w1[fb % 2], u[fb % 2]
            for n in range(4):
                for fc in range(4):
                    ph, r_ = psH[cnt["h"] % 2], rl[cnt["h"] % 2]
                    cnt["h"] += 1
                    for kc in range(8):
                        P.mm(ph[:], wb[:, kc, fc * 128:(fc + 1) * 128], hT[:, kc, n * 512:(n + 1) * 512], kc == 0, kc == 7,
                             wb.r + hT.r[n * 4:n * 4 + 4], ph.r)
                    P.act(r_[:], ph[:], AF.Relu, ph.r, r_.r)
                    P.tt("pool" if fc % 2 else "dve", ub[:, fc, n * 512:(n + 1) * 512], r_[:], r_[:], ALU.mult, r_.r, ub.r)

        pq = []

        def stage2(fb):
            wb, ub = w2[fb % 2], u[fb % 2]
            for t in range(NT):
                for half in range(2):
                    po = psO[cnt["o"] % 4]
                    cnt["o"] += 1
                    for fc in range(4):
                        P.mm(po[:], ub[:, fc, t * 128:(t + 1) * 128], wb[:, fc, half * 512:(half + 1) * 512], fc == 0, fc == 3,
                             ub.r + wb.r, po.r)
                    sl = slice(half * 512, (half + 1) * 512)
                    P.tt("dve", acc[:, t, sl], acc[:, t, sl], po[:], ALU.add, acc.r[t:t + 1] + po.r, acc.r[t:t + 1])
                if fb == 7:
                    if len(pq) >= 2:
                        pq.pop(0)()
                    v = _View(acc[:, t, :], acc.r[t:t + 1])
                    if last:
                        ln_tile(d, acc[:, t, :], acc.r[t:t + 1], t, out_dram=y_ap[t * 128:(t + 1) * 128, :], out_res=(),
                                to_hT=False, dst=v)
                    else:
                        pq.append(ln_tile(d, acc[:, t, :], acc.r[t:t + 1], t, out_dram=hres[t * 128:(t + 1) * 128, :],
                                          out_res=[P.r_hres[t]], dst=v))

        load_w(0)
        stage1(0)
        for fb in range(8):
            if fb + 1 < 8:
                load_w(fb + 1)
                stage1(fb + 1)
            stage2(fb)
        for f_ in pq:
            f_()
        S.flush()


def kernel(**inputs):
    inp = {k: np.asarray(v) for k, v in inputs.items()}
    shared = prep_shared(inp)
    P = build()
    x = np.ascontiguousarray(inp["x"], dtype=np.float32)
    in_maps = []
    for b in range(8):
        m = dict(shared)
        m["x"] = x[b]
        in_maps.append(m)
    res = run_bass_kernel_spmd(P.nc, in_maps, core_ids=list(range(8)))
    return np.stack([np.asarray(res.results[b]["y"], dtype=np.float32).reshape(S_TOK, D) for b in range(8)])
```

```python
import numpy as np
from contextlib import ExitStack
import concourse.bass as bass
import concourse.mybir as mybir
from concourse.bass_utils import run_bass_kernel_spmd

F32 = mybir.dt.float32
BF16 = mybir.dt.bfloat16
AF = mybir.ActivationFunctionType
ALU = mybir.AluOpType
AX = mybir.AxisListType

L = 2
S_TOK = 2048
D = 1024
NT = 16
NEG = -30000.0
LN_EPS = 1e-5
RMS_EPS = 1e-5
DN_ALPHA = float((2 * L) ** 0.25)

ENGS = ("pe", "act", "dve", "pool", "sp")
NSLOT = {"sp": 8, "pool": 8, "act": 4}


class Res:
    __slots__ = ("w", "r", "rd")

    def __init__(self):
        self.w = None
        self.r = {}
        self.rd = []


class Buf:
    def __init__(self, t, nres=1):
        self.t = t
        self.r = [Res() for _ in range(nres)]

    def __getitem__(self, k):
        return self.t[k]


class Sched:
    def __init__(self, nc):
        self.nc = nc
        self.es = ExitStack()
        self.csem = {e: self.es.enter_context(nc.semaphore("c_" + e)) for e in ENGS}
        self.dsem = {}
        for e, ns in NSLOT.items():
            for s in range(ns):
                self.dsem[(e, s)] = self.es.enter_context(nc.semaphore("d_%s%d" % (e, s)))
        self.cbase = {e: 0 for e in ENGS}
        self.dcnt = {e: 0 for e in NSLOT}
        self.batch = 0
        self.ops = {e: [] for e in ENGS}
        self.nops = 0

    def add(self, eng, fn, reads=(), writes=(), dma=False):
        idx = len(self.ops[eng])
        b = self.batch
        deps = set()
        for r in reads:
            if r.w is not None and r.w[0] == b:
                deps.add(r.w[1:])
        for w in writes:
            if w.w is not None and w.w[0] == b:
                deps.add(w.w[1:])
            for e2, (b2, i2) in w.r.items():
                if b2 == b:
                    deps.add((e2, i2))
            for x in w.rd:
                if x[0] == b:
                    deps.add(x[1:])
        deps.discard((eng, idx))
        for r in reads:
            if dma:
                r.rd.append((b, eng, idx))
            else:
                r.r[eng] = (b, idx)
        for w in writes:
            w.w = (b, eng, idx)
            w.r = {}
            w.rd = []
        self.ops[eng].append({"fn": fn, "deps": deps, "dma": dma, "inc": False})
        self.nops += 1

    def flush(self):
        nc = self.nc
        ops = self.ops
        for e in ENGS:
            for op in ops[e]:
                for (e2, i2) in op["deps"]:
                    t = ops[e2][i2]
                    if t["dma"]:
                        continue
                    if e2 == e and e == "pe":
                        continue
                    t["inc"] = True
            for op in reversed(ops[e]):
                if not op["dma"]:
                    op["inc"] = True
                    break
        cnt = {}
        final_c = {}
        for e in ENGS:
            c = self.cbase[e]
            for i, op in enumerate(ops[e]):
                if op["inc"] and not op["dma"]:
                    c += 1
                cnt[(e, i)] = c
            final_c[e] = c
        dslot = {}
        final_d = {}
        for e in ENGS:
            for i, op in enumerate(ops[e]):
                if op["dma"]:
                    ns = NSLOT[e]
                    j = self.dcnt[e]
                    dslot[(e, i)] = (e, j % ns, 16 * (j // ns + 1))
                    self.dcnt[e] = j + 1
        for e, ns in NSLOT.items():
            for s in range(ns):
                n = (self.dcnt[e] - s + ns - 1) // ns if self.dcnt[e] > s else 0
                final_d[(e, s)] = 16 * n
        csem, dsem = self.csem, self.dsem

        def run(e, eng):
            seen_c = dict(self.cbase)
            seen_d = {}
            for i, op in enumerate(ops[e]):
                waits_c = {}
                waits_d = {}
                for (e2, i2) in op["deps"]:
                    t = ops[e2][i2]
                    if t["dma"]:
                        se, sl, val = dslot[(e2, i2)]
                        k = (se, sl)
                        if seen_d.get(k, 0) < val:
                            waits_d[k] = max(waits_d.get(k, 0), val)
                    else:
                        if e2 == e and e == "pe":
                            continue
                        v = cnt[(e2, i2)]
                        if seen_c[e2] < v:
                            waits_c[e2] = max(waits_c.get(e2, 0), v)
                if op["dma"]:
                    se, sl, val = dslot[(e, i)]
                    if val > 16:
                        k = (se, sl)
                        if seen_d.get(k, 0) < val - 16:
                            waits_d[k] = max(waits_d.get(k, 0), val - 16)
                for e2, v in waits_c.items():
                    eng.wait_ge(csem[e2], v)
                    seen_c[e2] = v
                for k, v in waits_d.items():
                    eng.wait_ge(dsem[k], v)
                    seen_d[k] = v
                ins = op["fn"](eng)
                if op["dma"]:
                    se, sl, val = dslot[(e, i)]
                    ins.then_inc(dsem[(se, sl)], 16)
                elif op["inc"]:
                    ins.then_inc(csem[e], 1)
            for e2 in ENGS:
                if e2 != e and final_c[e2] > seen_c[e2]:
                    eng.wait_ge(csem[e2], final_c[e2])
            for k, v in final_d.items():
                if v > 0 and seen_d.get(k, 0) < v:
                    eng.wait_ge(dsem[k], v)

        with nc.Block() as block:
            @block.tensor
            def _(eng):
                run("pe", eng)

            @block.scalar
            def _(eng):
                run("act", eng)

            @block.vector
            def _(eng):
                run("dve", eng)

            @block.gpsimd
            def _(eng):
                run("pool", eng)

            @block.sync
            def _(eng):
                run("sp", eng)

        self.cbase = final_c
        self.batch += 1
        self.ops = {e: [] for e in ENGS}


def _rpb_tables(rpb_l):
    kc = np.arange(64)[:, None]
    qc = np.arange(64)[None, :]
    c0 = np.clip(qc - 8, 0, 48)
    colvalid = (kc >= c0) & (kc < c0 + 16)
    dcc = np.clip(kc - qc + 15, 0, 30)
    T = rpb_l[:, :, dcc]
    T = np.where(colvalid[None, None], T, np.float32(NEG)).astype(np.float32)
    out = np.full((16, 128, 2, 15, 64), NEG, np.float32)
    for var in range(2):
        for pos in range(15):
            dr = 14 - pos
            if var == 0 or 3 <= dr <= 10:
                out[:, 0:64, var, pos, :] = T[:, dr]
            dr1 = dr + 1
            if dr1 <= 14 and (var == 0 or 3 <= dr1 <= 10):
                out[:, 64:128, var, pos, :] = T[:, dr1]
    return out.reshape(16, 128, 2 * 15 * 64)


def _kmajor(w, kc):
    n = w.shape[1]
    return np.ascontiguousarray(w.reshape(kc, 128, n).transpose(1, 0, 2).reshape(128, kc * n))


def prep_shared(inp):
    f = np.float32
    o = {}
    o["lnp"] = np.stack([inp["ln0_g"], inp["ln0_b"]] +
                        [inp[k][l] for l in range(L) for k in ("ln1_g", "ln1_b", "ln2_g", "ln2_b")]).astype(f)
    j = np.arange(128)[:, None]
    l_ = np.arange(128)[None, :]
    o["c_ident"] = np.eye(128, dtype=f)
    o["c_uf"] = (j <= l_).astype(f)
    o["c_ub"] = (j >= l_).astype(f)
    o["c_mf"] = np.where(l_ >= j, 0.0, NEG).astype(f)
    o["c_mb"] = np.where(l_ <= j, 0.0, NEG).astype(f)
    sel = np.zeros((32, 32, 128), f)
    for r in range(32):
        sel[r, r, :] = 1.0
    o["c_sel"] = np.concatenate([sel.reshape(32, 32 * 128)] * 2, axis=0)
    last = np.zeros((128, 128), f)
    last[127, :] = 1.0
    first = np.zeros((128, 128), f)
    first[0, :] = 1.0
    o["c_last"] = last
    o["c_first"] = first
    w_in = inp["w_in"]
    wq, wg, wdt, wgate, rp = [], [], [], [], []
    for l in range(L):
        wl = w_in[l]
        a = []
        for hp in range(8):
            cols = np.concatenate([np.arange(hp * 128, hp * 128 + 128), 1024 + np.arange(hp * 128, hp * 128 + 128),
                                   2048 + np.arange(hp * 128, hp * 128 + 128)])
            a.append(_kmajor(wl[:, cols], 8))
        wq.append(np.stack(a))
        a = []
        for g in range(8):
            cols = np.concatenate([3072 + g * 256 + np.arange(256), 5120 + g * 256 + np.arange(256),
                                   7168 + g * 128 + np.arange(128), 8192 + g * 128 + np.arange(128)])
            a.append(_kmajor(wl[:, cols], 8))
        wg.append(np.stack(a))
        wdt.append(_kmajor(wl[:, 9216:9280], 8))
        wgate.append(np.stack([_kmajor(wl[:, 9280:10304], 8), _kmajor(wl[:, 10304:11328], 8)]))
        rp.append(_rpb_tables(inp["rpb"][l]))
    o["w_qkv"] = np.stack(wq)
    o["w_g"] = np.stack(wg)
    o["w_dt"] = np.stack(wdt)
    o["w_gate"] = np.stack(wgate)
    o["rpbt"] = np.stack(rp)
    o["w_br"] = np.stack([_kmajor(inp["w_attn_br"][l], 8) for l in range(L)])
    o["w_ssd"] = np.stack([_kmajor(inp["w_ssd_br"][l], 16) for l in range(L)])
    o["w_o"] = np.stack([_kmajor(inp["w_o"][l], 8) for l in range(L)])
    o["w_ff1"] = np.stack([np.stack([_kmajor(inp["w_ff1"][l][:, fb * 512:(fb + 1) * 512], 8) for fb in range(8)])
                           for l in range(L)])
    o["w_ff2"] = np.stack([np.stack([_kmajor(inp["w_ff2"][l][fb * 512:(fb + 1) * 512, :], 4) for fb in range(8)])
                           for l in range(L)])
    o["bgate"] = np.stack([np.ascontiguousarray(inp["b_gate"][l].reshape(16, 128).T) for l in range(L)])
    o["convw"] = np.stack([np.ascontiguousarray(inp["conv_w"][l].reshape(5, 32, 128).transpose(2, 1, 0).reshape(128, 160))
                           for l in range(L)])
    o["convb_f"] = np.stack([np.ascontiguousarray(inp["conv_b"][l].reshape(32, 128).T) for l in range(L)])
    cb_ = inp["conv_b"].reshape(L, 32, 128)
    o["convb_r"] = np.stack([np.stack([np.concatenate([np.tile(cb_[l, c_], 4) for c_ in (g * 2, g * 2 + 1, 16 + g, 24 + g)])[None, :]
                                       for g in range(8)]) for l in range(L)])
    o["alog"] = np.ascontiguousarray(inp["a_log"].reshape(L, 1, 64))
    o["dtb"] = np.ascontiguousarray(inp["dt_bias"].reshape(L, 1, 64))
    o["dskip"] = np.ascontiguousarray(inp["d_skip"].reshape(L, 1, 32))
    o["normw"] = np.stack([np.ascontiguousarray(inp["ssd_norm_w"][l].reshape(16, 128).T) for l in range(L)])
    return {k: np.ascontiguousarray(v, dtype=f) for k, v in o.items()}


IN_SHAPES = {
    "x": [S_TOK, D], "lnp": [2 + 4 * L, D],
    "c_ident": [128, 128], "c_uf": [128, 128], "c_ub": [128, 128], "c_mf": [128, 128], "c_mb": [128, 128],
    "c_sel": [64, 4096], "c_last": [128, 128], "c_first": [128, 128],
    "w_qkv": [L, 8, 128, 3072], "w_g": [L, 8, 128, 6144], "w_dt": [L, 128, 512], "w_gate": [L, 2, 128, 8192],
    "rpbt": [L, 16, 128, 1920], "w_br": [L, 128, 8192], "w_ssd": [L, 128, 16384], "w_o": [L, 128, 8192],
    "w_ff1": [L, 8, 128, 4096], "w_ff2": [L, 8, 128, 4096], "bgate": [L, 128, 16], "convw": [L, 128, 160],
    "convb_f": [L, 128, 32], "convb_r": [L, 8, 1, 2048], "alog": [L, 1, 64], "dtb": [L, 1, 64], "dskip": [L, 1, 32],
    "normw": [L, 128, 16],
}


class Prog:
    def __init__(self, dbg=()):
        self.nc = bass.Bass("TRN2", target_bir_lowering=False)
        self.S = Sched(self.nc)
        self.dbg = set(dbg)
        self.dbg_out = {}
        nc = self.nc
        self.din = {k: nc.dram_tensor(k, shp, F32, kind="ExternalInput") for k, shp in IN_SHAPES.items()}
        self.y = nc.dram_tensor("y", [S_TOK, D], F32, kind="ExternalOutput")
        self.hres = nc.dram_tensor("hres", [S_TOK, D], F32)
        self.gya = nc.dram_tensor("gya", [8, 128, S_TOK], BF16)
        self.yTd = nc.dram_tensor("yTd", [16, 128, S_TOK], BF16)
        self.r_hres = [Res() for _ in range(NT)]
        self.r_gya = [Res() for _ in range(4)]
        self.r_yTd = [Res() for _ in range(8)]

    def sb(self, st, name, shape, dt, nres=1):
        self.uid = getattr(self, "uid", 0) + 1
        nb = int(np.prod(shape[1:])) * (4 if dt == F32 else 2)
        nb = (nb + 31) // 32 * 32
        self.cur = getattr(self, "cur", 0) + nb
        self.hwm = max(getattr(self, "hwm", 0), self.cur)
        st.callback(lambda: setattr(self, "cur", self.cur - nb))
        return Buf(st.enter_context(self.nc.sbuf_tensor("s%d_%s" % (self.uid, name), shape, dt)), nres)

    def ps(self, st, name, shape, dt, nres=1):
        self.uid = getattr(self, "uid", 0) + 1
        return Buf(st.enter_context(self.nc.psum_tensor("p%d_%s" % (self.uid, name), shape, dt)), nres)

    def mm(self, out, lhsT, rhs, start, stop, reads, writes, **kw):
        self.S.add("pe", lambda e: e.matmul(out, lhsT=lhsT, rhs=rhs, start=start, stop=stop, **kw), reads, writes)

    def tr(self, out, in_, ident, reads, writes):
        self.S.add("pe", lambda e: e.transpose(out=out, in_=in_, identity=ident), reads, writes)

    def act(self, out, in_, func, reads, writes, bias=None, scale=None, accum_out=None, eng="act"):
        kw = {}
        if bias is not None:
            kw["bias"] = bias
        if scale is not None:
            kw["scale"] = scale
        if accum_out is not None:
            kw["accum_out"] = accum_out
        self.S.add(eng, lambda e: e.activation(out=out, in_=in_, func=func, **kw), reads, writes)

    def copy(self, eng, out, in_, reads, writes):
        if eng == "act":
            self.S.add("act", lambda e: e.copy(out=out, in_=in_), reads, writes)
        else:
            self.S.add(eng, lambda e: e.tensor_copy(out=out, in_=in_), reads, writes)

    def tt(self, eng, out, in0, in1, op, reads, writes):
        self.S.add(eng, lambda e: e.tensor_tensor(out=out, in0=in0, in1=in1, op=op), reads, writes)

    def ts(self, eng, out, in0, s1, s2, op0, op1, reads, writes):
        if op1 is None:
            self.S.add(eng, lambda e: e.tensor_scalar(out=out, in0=in0, scalar1=s1, scalar2=None, op0=op0), reads, writes)
        else:
            self.S.add(eng, lambda e: e.tensor_scalar(out=out, in0=in0, scalar1=s1, scalar2=s2, op0=op0, op1=op1),
                       reads, writes)

    def stt(self, eng, out, in0, scalar, in1, op0, op1, reads, writes):
        self.S.add(eng, lambda e: e.scalar_tensor_tensor(out=out, in0=in0, scalar=scalar, in1=in1, op0=op0, op1=op1),
                   reads, writes)

    def memset(self, eng, ap, val, writes):
        self.S.add(eng, lambda e: e.memset(ap, val), (), writes)

    def dma(self, eng, out, in_, reads, writes):
        self.S.add(eng, lambda e: e.dma_start(out=out, in_=in_), reads, writes, dma=True)

    def stager(self, st, n=3):
        bufs = [self.sb(st, "stg%d" % i, [128, 2048], F32) for i in range(n)]
        state = {"i": 0}
        engs = ("act", "dve", "pool")

        def load(dst2d, src2d, dst_res, eng=None):
            i = state["i"]
            state["i"] += 1
            b = bufs[i % n]
            self.dma("sp", b[:], src2d, (), b.r)
            self.copy(eng or engs[i % 2], dst2d, b[:], b.r, dst_res)
        return load

    def dump(self, name, ap_sb, shape, dt, reads):
        t = self.nc.dram_tensor("dbg_" + name, shape, dt, kind="ExternalOutput")
        self.dbg_out["dbg_" + name] = t
        self.dma("sp", t.ap(), ap_sb, reads, ())


def rsqrt_cols(P, out, v, tmp, res):
    P.act(tmp, v, AF.Ln, res, res)
    P.act(out, tmp, AF.Exp, res, res, scale=-0.5)
    P.tt("dve", tmp, v, out, ALU.mult, res, res)
    P.tt("dve", tmp, tmp, out, ALU.mult, res, res)
    P.ts("dve", tmp, tmp, -0.5, 1.5, ALU.mult, ALU.add, res, res)
    P.tt("dve", out, out, tmp, ALU.mult, res, res)


def build(dbg=(), stop_after=None, nlayers=L):
    P = Prog(dbg)
    P.stop = stop_after
    nc, S = P.nc, P.S
    din = P.din
    G = ExitStack()

    identf = P.sb(G, "identf", [128, 128], F32)
    identb = P.sb(G, "identb", [128, 128], BF16)
    onesb = P.sb(G, "onesb", [128, 128], BF16)
    hT = P.sb(G, "hT", [128, 8, S_TOK], BF16, nres=NT)
    P.dma("sp", identf[:], din["c_ident"].ap(), (), identf.r)
    P.copy("dve", identb[:], identf[:], identf.r, identb.r)
    P.memset("dve", onesb[:], 1.0, onesb.r)

    def hT_r(t0, n=1):
        return hT.r[t0:t0 + n]

    def ln_phase_bufs(st, with_xn=True):
        d = {}
        d["gb"] = P.sb(st, "ln_gb", [128, 2, D], F32)
        d["st"] = [P.sb(st, "ln_st%d" % i, [128, 12], F32) for i in range(2)]
        d["mv"] = [P.sb(st, "ln_mv%d" % i, [128, 4], F32) for i in range(2)]
        if with_xn:
            d["xn"] = [P.sb(st, "ln_xn%d" % i, [128, D], F32) for i in range(2)]
        d["hb"] = [P.sb(st, "ln_hb%d" % i, [128, D], BF16) for i in range(2)]
        d["pT"] = [P.ps(st, "ln_pT%d" % i, [128, 8, 128], BF16) for i in range(2)]
        d["n"] = 0
        return d

    def ln_load_params(d, row):
        lnp = din["lnp"].ap()
        P.dma("sp", d["gb"][:, 0, :], lnp[row:row + 1, :].partition_broadcast(128), (), d["gb"].r)
        P.dma("sp", d["gb"][:, 1, :], lnp[row + 1:row + 2, :].partition_broadcast(128), (), d["gb"].r)

    def ln_tile(d, src_ap, src_res, t, out_dram=None, out_res=None, to_hT=True, dst=None):
        i = d["n"] % 2
        d["n"] += 1
        stt_, mv, hb, pT = d["st"][i], d["mv"][i], d["hb"][i], d["pT"][i]
        xn = dst if dst is not None else d["xn"][i]
        ho = xn
        S.add("dve", lambda e: e.bn_stats(stt_[:, 0:6], src_ap[:, 0:512]), src_res, stt_.r)
        S.add("dve", lambda e: e.bn_stats(stt_[:, 6:12], src_ap[:, 512:1024]), src_res, stt_.r)
        S.add("dve", lambda e: e.bn_aggr(mv[:, 0:2], stt_[:, 0:12].rearrange("p (t j) -> p t j", j=3)), stt_.r, mv.r)
        P.ts("dve", mv[:, 1:2], mv[:, 1:2], LN_EPS, None, ALU.add, None, mv.r, mv.r)
        rsqrt_cols(P, mv[:, 2:3], mv[:, 1:2], mv[:, 3:4], mv.r)
        P.stt("dve", xn[:], src_ap, mv[:, 0:1], d["gb"][:, 0, :], ALU.subtract, ALU.mult,
              list(src_res) + mv.r + d["gb"].r, xn.r)
        P.stt("dve", xn[:], xn[:], mv[:, 2:3], d["gb"][:, 1, :], ALU.mult, ALU.add, xn.r + mv.r + d["gb"].r, xn.r)
        if out_dram is not None:
            P.dma("pool", out_dram, ho[:], ho.r, out_res)
        if not to_hT:
            return None
        P.copy("act", hb[:], ho[:], ho.r, hb.r)

        def later():
            for kc in range(8):
                P.tr(pT[:, kc, :], hb[:, kc * 128:(kc + 1) * 128], identb[:], hb.r + identb.r, pT.r)
            P.copy("act", hT[:, :, t * 128:(t + 1) * 128], pT[:], pT.r, hT_r(t))
        return later

    with ExitStack() as st:
        d = ln_phase_bufs(st)
        ln_load_params(d, 0)
        xin = [P.sb(st, "xin%d" % i, [128, D], F32) for i in range(3)]
        x_ap = din["x"].ap()
        hres_ap = P.hres.ap()
        pq = []
        for t in range(NT):
            xb = xin[t % 3]
            P.dma("sp", xb[:], x_ap[t * 128:(t + 1) * 128, :], (), xb.r)
            if len(pq) >= 2:
                pq.pop(0)()
            pq.append(ln_tile(d, xb[:], xb.r, t, out_dram=hres_ap[t * 128:(t + 1) * 128, :], out_res=[P.r_hres[t]]))
        for f_ in pq:
            f_()
        if "hT0" in P.dbg:
            P.dump("hT0", hT[:], [128, 8, S_TOK], BF16, hT.r)
        S.flush()
    if stop_after == "ln0":
        return P

    for l in range(nlayers):
        attention_phase(P, l, hT, identb, onesb)
        if stop_after == "att%d" % l:
            return P
        ssd_phase(P, l, hT, identb, identf, onesb)
        if stop_after in ("ssd%d" % l, "ssdprep%d" % l):
            return P
        merge_phase(P, l, hT, identb, ln_phase_bufs, ln_load_params, ln_tile)
        if stop_after == "merge%d" % l:
            return P
        ffn_phase(P, l, hT, identb, ln_phase_bufs, ln_load_params, ln_tile, last=(l == nlayers - 1))
        if stop_after == "ffn%d" % l:
            return P
    return P


def attention_phase(P, l, hT, identb, onesb):
    S = P.S
    din = P.din
    with ExitStack() as so:
      ao = P.sb(so, "ao", [128, NT, D], BF16, nres=NT)
      with ExitStack() as st:
        wqkv = [P.sb(st, "wqkv%d" % i, [128, 8, 384], BF16) for i in range(2)]
        tab = [P.sb(st, "tab%d" % i, [128, 1920], BF16) for i in range(2)]
        QT = P.sb(st, "QT", [128, S_TOK], BF16)
        KT = P.sb(st, "KT", [128, S_TOK], BF16)
        Vt = P.sb(st, "Vt", [128, NT, 2, 65], BF16)
        PT = [P.sb(st, "PT%d" % i, [128, 512], BF16) for i in range(4)]
        PR = [P.sb(st, "PR%d" % i, [128, 512], BF16) for i in range(4)]
        rc = [P.sb(st, "rc%d" % i, [128, 2], F32) for i in range(2)]
        psA = [P.ps(st, "psA%d" % i, [128, 512], F32) for i in range(2)]
        psS = [P.ps(st, "psS%d" % i, [128, 512], F32) for i in range(4)]
        psO = [P.ps(st, "psO%d" % i, [128, 512], F32) for i in range(2)]
        P.memset("dve", Vt[:, :, :, 64:65], 1.0, Vt.r)
        w_qkv = din["w_qkv"].ap()
        rpbt = din["rpbt"].ap()
        n_s = 0
        n_pt = 0
        n_o = 0
        n_a = 0
        for hp in range(8):
            wb = wqkv[hp % 2]
            P.dma("pool", wb[:].rearrange("p a b -> p (a b)"), w_qkv[l, hp], (), wb.r)
            for which in range(2):
                for n in range(4):
                    ps = psA[n_a % 2]
                    n_a += 1
                    for kc in range(8):
                        P.mm(ps[:], wb[:, kc, which * 128:(which + 1) * 128], hT[:, kc, n * 512:(n + 1) * 512],
                             kc == 0, kc == 7, wb.r + hT.r[n * 4:n * 4 + 4], ps.r)
                    if which == 0:
                        P.act(QT[:, n * 512:(n + 1) * 512], ps[:], AF.Copy, ps.r, QT.r, scale=0.125)
                    else:
                        P.copy("dve", KT[:, n * 512:(n + 1) * 512], ps[:], ps.r, KT.r)
            for t4 in range(4):
                ps = psA[n_a % 2]
                n_a += 1
                for tt in range(4):
                    t = t4 * 4 + tt
                    for kc in range(8):
                        P.mm(ps[:, tt * 128:(tt + 1) * 128], hT[:, kc, t * 128:(t + 1) * 128], wb[:, kc, 256:384],
                             kc == 0, kc == 7, wb.r + hT.r[t:t + 1], ps.r)
                P.copy("dve", Vt[:, t4 * 4:(t4 + 1) * 4, :, 0:64],
                       ps[:].rearrange("p (t h d) -> p t h d", t=4, h=2), ps.r, Vt.r)
            items = []
            for hh in range(2):
                h = 2 * hp + hh
                tb = tab[h % 2]
                P.dma("pool", tb[:], rpbt[l, h], (), tb.r)
                P.act(tb[:], tb[:], AF.Exp, tb.r, tb.r)
                for qb in range(8):
                    R = qb * 4
                    if qb == 0:
                        krs, var = [0, 2, 4, 6], 0
                    elif qb == 7:
                        krs, var = [24, 26, 28, 30], 0
                    else:
                        krs, var = [R - 4, R - 2, R, R + 2, R + 4, R + 6], 1
                    pO = psO[n_o % 2]
                    n_o += 1
                    r_ = rc[n_o % 2]
                    for pi in range(len(krs) // 2):
                        items.append({"hh": hh, "h": h, "tb": tb, "qb": qb, "R": R, "var": var, "krs": krs, "pi": pi,
                                      "pO": pO, "rc": r_})

            def stage_a(it):
                nonlocal n_s
                ps = psS[n_s % 4]
                n_s += 1
                it["ps"] = ps
                po, R = it["hh"] * 64, it["R"]
                for j in range(2):
                    kr0 = it["krs"][2 * it["pi"] + j]
                    P.mm(ps[:, j * 256:(j + 1) * 256], KT[po:po + 64, kr0 * 64:kr0 * 64 + 128],
                         QT[po:po + 64, R * 64:R * 64 + 256], True, True, KT.r + QT.r, ps.r)

            def stage_b(it):
                nonlocal n_pt
                ps, tb = it["ps"], it["tb"]
                pt = PT[n_pt % 4]
                pr = PR[n_pt % 4]
                n_pt += 1
                it["pt"] = pt
                P.act(pr[:], ps[:], AF.Exp, ps.r, pr.r)
                for j in range(2):
                    kr0 = it["krs"][2 * it["pi"] + j]
                    c0 = it["var"] * 960 + (7 - (kr0 - it["R"])) * 64
                    P.tt("dve", pt[:, j * 256:(j + 1) * 256], pr[:, j * 256:(j + 1) * 256], tb[:, c0:c0 + 256], ALU.mult,
                         pr.r + tb.r, pt.r)

            def stage_c(it):
                pt, pO, hh, h, qb = it["pt"], it["pO"], it["hh"], it["h"], it["qb"]
                nk = len(it["krs"])
                for j in range(2):
                    ki = 2 * it["pi"] + j
                    kr0 = it["krs"][ki]
                    for half in range(2):
                        P.mm(pO[:, half * 128:half * 128 + 65], pt[:, j * 256 + half * 128:j * 256 + (half + 1) * 128],
                             Vt[:, kr0 // 2, hh, :], ki == 0 and half == 0, ki == nk - 1, pt.r + Vt.r, pO.r,
                             skip_group_check=True)
                if it["pi"] == nk // 2 - 1:
                    r_ = it["rc"]
                    for half in range(2):
                        t = qb * 2 + half
                        S.add("dve", (lambda e, r_=r_, half=half, pO=pO: e.reciprocal(r_[:, half:half + 1],
                                                                                      pO[:, half * 128 + 64:half * 128 + 65])),
                              pO.r, r_.r)
                        P.ts("dve", ao[:, t, h * 64:(h + 1) * 64], pO[:, half * 128:half * 128 + 64], r_[:, half:half + 1], None,
                             ALU.mult, None, pO.r + r_.r, ao.r[t:t + 1])

            LAG = 3
            for k in range(len(items) + LAG):
                if k < len(items):
                    stage_a(items[k])
                    stage_b(items[k])
                if k >= LAG:
                    stage_c(items[k - LAG])
        if "ao%d" % l in P.dbg:
            P.dump("ao%d" % l, ao[:], [128, NT, D], BF16, ao.r)
        S.flush()
      if True:
        with ExitStack() as st2:
            aoT = P.sb(st2, "aoT", [128, 8, S_TOK], BF16, nres=NT)
            wbr = P.sb(st2, "wbr", [128, 8, D], BF16)
            wga = P.sb(st2, "wga", [128, 8, D], BF16)
            bg = P.sb(st2, "bg", [128, 16], F32)
            gs = [P.sb(st2, "gs%d" % i, [128, 512], F32) for i in range(2)]
            stage = [P.sb(st2, "gyast%d" % i, [128, 8, 512], BF16) for i in range(2)]
            pT = [P.ps(st2, "a2pT%d" % i, [128, 8, 128], BF16) for i in range(2)]
            psY = [P.ps(st2, "psY%d" % i, [128, 512], F32) for i in range(2)]
            psG = [P.ps(st2, "psG%d" % i, [128, 512], F32) for i in range(2)]
            ld = P.stager(st2)
            for q in range(4):
                ld(wbr[:, q * 2:(q + 1) * 2, :].rearrange("p a b -> p (a b)"), din["w_br"].ap()[l, :, q * 2048:(q + 1) * 2048], wbr.r)
                ld(wga[:, q * 2:(q + 1) * 2, :].rearrange("p a b -> p (a b)"), din["w_gate"].ap()[l, 0, :, q * 2048:(q + 1) * 2048],
                   wga.r)
            P.dma("sp", bg[:], din["bgate"].ap()[l], (), bg.r)
            for t in range(NT):
                p = pT[t % 2]
                for kc in range(8):
                    P.tr(p[:, kc, :], ao[:, t, kc * 128:(kc + 1) * 128], identb[:], ao.r[t:t + 1] + identb.r, p.r)
                P.copy("act" if t % 2 else "dve", aoT[:, :, t * 128:(t + 1) * 128], p[:], p.r, aoT.r[t:t + 1])
            gya_ap = P.gya.ap()
            k = 0
            for n in range(4):
                sg = stage[n % 2]
                for c in range(8):
                    py, pg, g_ = psY[k % 2], psG[k % 2], gs[k % 2]
                    k += 1
                    for kc in range(8):
                        P.mm(py[:], wbr[:, kc, c * 128:(c + 1) * 128], aoT[:, kc, n * 512:(n + 1) * 512],
                             kc == 0, kc == 7, wbr.r + aoT.r[n * 4:n * 4 + 4], py.r)
                    for kc in range(8):
                        P.mm(pg[:], wga[:, kc, c * 128:(c + 1) * 128], hT[:, kc, n * 512:(n + 1) * 512],
                             kc == 0, kc == 7, wga.r + hT.r[n * 4:n * 4 + 4], pg.r)
                    P.act(g_[:], pg[:], AF.Sigmoid, pg.r + bg.r, g_.r, bias=bg[:, c:c + 1])
                    P.tt("dve", sg[:, c, :], py[:], g_[:], ALU.mult, py.r + g_.r, sg.r)
                P.dma("sp", gya_ap[:, :, n * 512:(n + 1) * 512].rearrange("c p t -> p c t"), sg[:], sg.r, [P.r_gya[n]])
            S.flush()


def ssd_phase(P, l, hT, identb, identf, onesb):
    S = P.S
    din = P.din
    f32b = lambda st, name, shape, nres=1: P.sb(st, name, shape, F32, nres)
    with ExitStack() as so:
        uf = f32b(so, "uf", [128, 128])
        ub = f32b(so, "ub", [128, 128])
        lastm = f32b(so, "lastm", [128, 128])
        firstm = f32b(so, "firstm", [128, 128])
        sel = P.sb(so, "sel", [64, 512], BF16)
        mask = [P.sb(so, "mask%d" % i, [128, 4, 128], BF16) for i in range(2)]
        bias_tok = f32b(so, "bias_tok", [128, NT, 64])
        e_tok = f32b(so, "e_tok", [128, NT, 64])
        dtdec = f32b(so, "dtdec", [128, NT, 64])
        decay_bc = f32b(so, "decay_bc", [128, NT, 64])
        acsHL = [P.sb(so, "acsHL%d" % i, [64, S_TOK], BF16) for i in range(2)]
        dskip_bc = f32b(so, "dskip_bc", [128, 32])
        normw16 = f32b(so, "normw16", [128, 16])
        convw = f32b(so, "convw", [128, 160])
        convb_f = f32b(so, "convb_f", [128, 32])
        convb_r = P.sb(so, "convb_r", [1, 2048], BF16)
        P.dma("sp", uf[:], din["c_uf"].ap(), (), uf.r)
        P.dma("sp", ub[:], din["c_ub"].ap(), (), ub.r)
        P.dma("sp", lastm[:], din["c_last"].ap(), (), lastm.r)
        P.dma("sp", firstm[:], din["c_first"].ap(), (), firstm.r)
        P.dma("sp", dskip_bc[:], din["dskip"].ap()[l].partition_broadcast(128), (), dskip_bc.r)
        P.dma("sp", normw16[:], din["normw"].ap()[l], (), normw16.r)
        P.dma("sp", convw[:], din["convw"].ap()[l], (), convw.r)
        P.dma("sp", convb_f[:], din["convb_f"].ap()[l], (), convb_f.r)
        P.ts("dve", normw16[:], normw16[:], 16.0, None, ALU.mult, None, normw16.r, normw16.r)
        with ExitStack() as st:
            mtmp = f32b(st, "mtmp", [128, 128])
            wdt = P.sb(st, "wdt", [128, 8, 64], BF16)
            dtb_bc = f32b(st, "dtb_bc", [128, 64])
            a_bc = f32b(st, "a_bc", [128, 64])
            x_ = f32b(st, "x_", [128, NT, 64])
            dt_ = f32b(st, "dt_", [128, NT, 64])
            adt = f32b(st, "adt", [128, NT, 64])
            acs = f32b(st, "acs", [128, NT, 64])
            last_bc = f32b(st, "last_bc", [128, NT, 64])
            tmp = f32b(st, "tmp", [128, NT, 64])
            psD = [P.ps(st, "psD%d" % i, [128, 8, 64], F32) for i in range(2)]
            psC = [P.ps(st, "psC%d" % i, [128, 8, 64], F32) for i in range(2)]
            psT = [P.ps(st, "psT%d" % i, [64, 512], F32) for i in range(2)]
            adtd = f32b(st, "adtd", [128, NT, 2, 64])
            psL = [P.ps(st, "psL%d" % i, [128, NT, 32], F32) for i in range(2)]
            for i, nm in enumerate(("c_mf", "c_mb")):
                P.dma("sp", mtmp[:], din[nm].ap(), (), mtmp.r)
                for cc in range(4):
                    P.copy("dve", mask[i][:, cc, :], mtmp[:], mtmp.r, mask[i].r)
            P.dma("pool", wdt[:].rearrange("p a b -> p (a b)"), din["w_dt"].ap()[l], (), wdt.r)
            P.dma("sp", dtb_bc[:], din["dtb"].ap()[l].partition_broadcast(128), (), dtb_bc.r)
            P.dma("sp", a_bc[:], din["alog"].ap()[l].partition_broadcast(128), (), a_bc.r)
            P.act(a_bc[:], a_bc[:], AF.Exp, a_bc.r, a_bc.r)
            P.ts("dve", a_bc[:], a_bc[:], -1.0, None, ALU.mult, None, a_bc.r, a_bc.r)
            for half in range(2):
                for tt in range(8):
                    t = half * 8 + tt
                    for kc in range(8):
                        P.mm(psD[half][:, tt, :], hT[:, kc, t * 128:(t + 1) * 128], wdt[:, kc, :], kc == 0, kc == 7,
                             hT.r[t:t + 1] + wdt.r, psD[half].r)
                P.tt("dve", x_[:, half * 8:(half + 1) * 8, :], psD[half][:],
                     dtb_bc[:].unsqueeze(1).broadcast_to([128, 8, 64]), ALU.add, psD[half].r + dtb_bc.r, x_.r)
            P.act(tmp[:], x_[:], AF.Exp, x_.r, tmp.r)
            P.act(dt_[:], tmp[:], AF.Ln, tmp.r, dt_.r, bias=1.0)
            P.tt("dve", adt[:], dt_[:], a_bc[:].unsqueeze(1).broadcast_to([128, NT, 64]), ALU.mult, dt_.r + a_bc.r, adt.r)
            for c in range(NT):
                pc = psC[c // 8]
                P.mm(pc[:, c % 8, 0:32], uf[:], adt[:, c, 0:32], True, True, uf.r + adt.r, pc.r)
                P.mm(pc[:, c % 8, 32:64], ub[:], adt[:, c, 32:64], True, True, ub.r + adt.r, pc.r)
            for half in range(2):
                P.copy("dve", acs[:, half * 8:(half + 1) * 8, :], psC[half][:], psC[half].r, acs.r)
            k = 0
            for d_ in range(2):
                for hf_ in range(2):
                    P.copy("dve", adtd[:, :, d_, hf_ * 32:(hf_ + 1) * 32], adt[:, :, d_ * 32:(d_ + 1) * 32], adt.r, adtd.r)
            for d_ in range(2):
                um = uf if d_ == 0 else ub
                for cb in range(4):
                    pt = psT[k % 2]
                    k += 1
                    for cc in range(4):
                        c = cb * 4 + cc
                        P.mm(pt[:, cc * 128:(cc + 1) * 128], adtd[:, c, d_, :], um[:], True, True,
                             adtd.r + um.r, pt.r)
                    blk_ = slice(cb * 512, (cb + 1) * 512)
                    P.copy("act", acsHL[d_][:, blk_], pt[:], pt.r, acsHL[d_].r)
                    P.tt("dve", acsHL[d_][32:64, blk_], pt[32:64, :], acsHL[d_][32:64, blk_], ALU.subtract,
                         pt.r + acsHL[d_].r, acsHL[d_].r)
            P.mm(psL[0][:], lastm[:], acs[:, :, 0:32], True, True, lastm.r + acs.r, psL[0].r)
            P.mm(psL[1][:], firstm[:], acs[:, :, 32:64], True, True, firstm.r + acs.r, psL[1].r)
            for d_ in range(2):
                P.copy("dve", last_bc[:, :, d_ * 32:(d_ + 1) * 32], psL[d_][:], psL[d_].r, last_bc.r)
            P.act(decay_bc[:], last_bc[:], AF.Exp, last_bc.r, decay_bc.r)
            P.tt("dve", tmp[:], last_bc[:], acs[:], ALU.subtract, last_bc.r + acs.r, tmp.r)
            P.act(tmp[:], tmp[:], AF.Exp, tmp.r, tmp.r)
            P.tt("dve", dtdec[:], dt_[:], tmp[:], ALU.mult, dt_.r + tmp.r, dtdec.r)
            P.act(e_tok[:], acs[:], AF.Exp, acs.r, e_tok.r)
            P.act(tmp[:], dt_[:], AF.Ln, dt_.r, tmp.r)
            P.tt("dve", bias_tok[:], tmp[:], acs[:], ALU.subtract, tmp.r + acs.r, bias_tok.r)
            if "dt%d" % l in P.dbg:
                P.dump("dt%d" % l, dt_[:], [128, NT, 64], F32, dt_.r)
                P.dump("acs%d" % l, acs[:], [128, NT, 64], F32, acs.r)
            S.flush()
        if P.stop == "ssdprep%d" % l:
            return
        with ExitStack() as st:
            Wgs = [P.sb(st, "Wg%d" % i, [128, 8, 768], BF16) for i in range(2)]
            uT = [P.sb(st, "uT%d" % i, [128, 2052], BF16) for i in range(2)]
            zs = P.sb(st, "zs", [128, NT, 256], BF16, nres=NT)
            xs_bf = P.sb(st, "xs_bf", [128, NT, 256], BF16)
            xdec = [P.sb(st, "xdec%d" % i, [128, 2, 256], BF16) for i in range(2)]
            Bt = P.sb(st, "Bt", [128, NT, 128], BF16)
            BT = P.sb(st, "BT", [128, S_TOK], BF16)
            CT = P.sb(st, "CT", [128, S_TOK], BF16)
            CBT = P.sb(st, "CBT", [128, NT, 128], BF16)
            prevb = [P.sb(st, "prevb%d" % i, [128, NT, 256], BF16) for i in range(2)]
            Sst = [[f32b(st, "Sst%d_%d" % (i, j), [128, 256]) for j in range(2)] for i in range(2)]
            E = [P.sb(st, "E%d" % i, [128, 512], BF16) for i in range(2)]
            MT = [[P.sb(st, "MT%d_%d" % (i, d_), [128, 512], BF16) for d_ in range(2)] for i in range(2)]
            dgs = [P.sb(st, "dg%d" % i, [128, 4, 5, 128], BF16) for i in range(2)]
            Yb = [f32b(st, "Yb%d" % i, [128, 256]) for i in range(2)]
            Yg = [f32b(st, "Yg%d" % i, [128, 4, 256]) for i in range(2)]
            ss = [f32b(st, "ss%d" % i, [128, 12]) for i in range(2)]
            junk = f32b(st, "junk", [128, 256])
            yTst = [P.sb(st, "yTst%d" % i, [128, 2, 512], BF16) for i in range(2)]
            yoS = [f32b(st, "yoS%d" % i, [128, 512]) for i in range(2)]
            Yn = [P.sb(st, "Yn%d" % i, [128, 4, 256], BF16) for i in range(2)]
            pend = [None]
            bk = [P.ps(st, "bk%d" % i, [128, 512], F32) for i in range(7)]
            psTr = [P.ps(st, "psTr", [128, 2, 4, 128], BF16)]
            for u in uT:
                P.memset("dve", u[:, 0:2], 0.0, u.r)
                P.memset("dve", u[:, 2050:2052], 0.0, u.r)
            yTd = P.yTd.ap()
            w_g = din["w_g"].ap()
            cnt = {"u": 0, "x": 0, "f": 0, "z": 0, "seg": 0, "e": 0, "blk": 0, "yn": 0}
            Dsk = P.sb(st, "Dsk", [128, 4, 128], BF16)
            T1 = [f32b(st, "T1_%d" % i, [128, 512]) for i in range(2)]
            lagq = []

            def flush_lag():
                q = list(lagq)
                del lagq[:]
                for f_ in q:
                    f_()

            ngroups = 1 if "ssd_g1" in P.dbg else 8
            P.dma("pool", Wgs[0][:].rearrange("p a b -> p (a b)"), w_g[l, 0], (), Wgs[0].r)
            for g in range(ngroups):
                Wg = Wgs[g % 2]
                if g + 1 < ngroups:
                    P.dma("pool", Wgs[(g + 1) % 2][:].rearrange("p a b -> p (a b)"), w_g[l, g + 1], (), Wgs[(g + 1) % 2].r)
                P.dma("pool", sel[:], din["c_sel"].ap()[:, g * 512:(g + 1) * 512], (), sel.r)
                P.dma("pool", convb_r[:], din["convb_r"].ap()[l, g], (), convb_r.r)
                chs = [g * 2, g * 2 + 1, 16 + g, 24 + g]

                dg = dgs[g % 2]

                def build_dg(gg, part=None):
                    cs = [gg * 2, gg * 2 + 1, 16 + gg, 24 + gg]
                    dd = dgs[gg % 2]
                    for k in range(4):
                        for j in range(5):
                            if part is not None and (k * 5 + j) // 2 != part:
                                continue
                            col = cs[k] * 5 + j
                            P.act(dd[:, k, j, :], identf[:], AF.Copy, identf.r + convw.r, dd.r, scale=convw[:, col:col + 1])

                if g == 0:
                    build_dg(0)
                for r in range(4):
                    P.act(Dsk[:, r, :], identf[:], AF.Copy, identf.r + dskip_bc.r, Dsk.r,
                          scale=dskip_bc[:, g * 4 + r:g * 4 + r + 1])

                def proj_u(k):
                    u = uT[k % 2]
                    c0 = 256 + k * 128
                    for n in range(4):
                        b = bk[3 + cnt["u"] % 2]
                        cnt["u"] += 1
                        for kc in range(8):
                            P.mm(b[:], Wg[:, kc, c0:c0 + 128], hT[:, kc, n * 512:(n + 1) * 512], kc == 0, kc == 7,
                                 Wg.r + hT.r[n * 4:n * 4 + 4], b.r)
                        P.copy("dve" if (n % 2 and k < 3) else "act", u[:, 2 + n * 512:2 + (n + 1) * 512], b[:], b.r, u.r)

                def conv_tok(k, t4):
                    u = uT[k % 2]
                    b = bk[5 + cnt["x"] % 2]
                    cnt["x"] += 1
                    P.mm(b[:], onesb[0:1, 0:128], convb_r[0:1, k * 512:(k + 1) * 512], True, False,
                         onesb.r + convb_r.r, b.r)
                    for tt in range(4):
                        t = t4 * 4 + tt
                        o = b[:, tt * 128:(tt + 1) * 128]
                        for j in range(5):
                            P.mm(o, u[:, t * 128 + j:t * 128 + j + 128], dg[:, k, j, :], False, tt == 3 and j == 4,
                                 u.r + dg.r, b.r)
                    if k < 2:
                        P.act(xs_bf[:, t4 * 4:(t4 + 1) * 4, k * 128:(k + 1) * 128],
                              b[:].rearrange("p (t c) -> p t c", t=4), AF.Silu, b.r, xs_bf.r)
                    else:
                        P.act(Bt[:, t4 * 4:(t4 + 1) * 4, :], b[:].rearrange("p (t c) -> p t c", t=4), AF.Silu, b.r, Bt.r)

                def conv_feat(k, n):
                    u = uT[k % 2]
                    dst = BT if k == 2 else CT
                    b = (bk[1], bk[2])[cnt["f"] % 2] if k == 2 else bk[4]
                    cnt["f"] += 1
                    for j in range(5):
                        P.mm(b[:], dg[:, k, j, :], u[:, n * 512 + j:n * 512 + j + 512], j == 0, j == 4, u.r + dg.r, b.r)
                    P.act(dst[:, n * 512:(n + 1) * 512], b[:], AF.Silu, b.r + convb_f.r, dst.r,
                          bias=convb_f[:, chs[k]:chs[k] + 1])

                def z_tile(t):
                    b = bk[cnt["z"] % 2]
                    cnt["z"] += 1
                    for kc in range(8):
                        P.mm(b[:, 0:256], hT[:, kc, t * 128:(t + 1) * 128], Wg[:, kc, 0:256],
                             kc == 0, kc == 7, hT.r[t:t + 1] + Wg.r, b.r)
                    P.act(zs[:, t, :], b[:, 0:256], AF.Silu, b.r, zs.r[t:t + 1])

                def cbt_block(cb):
                    b = bk[4]
                    for cc in range(4):
                        c = cb * 4 + cc
                        P.mm(b[:, cc * 128:(cc + 1) * 128], BT[:, c * 128:(c + 1) * 128], CT[:, c * 128:(c + 1) * 128],
                             True, True, BT.r + CT.r, b.r)
                    P.copy("act", CBT[:, cb * 4:(cb + 1) * 4, :], b[:].rearrange("p (t c) -> p t c", t=4), b.r, CBT.r)

                def scan_step(i):
                    xd = xdec[i % 2]
                    for d_ in range(2):
                        c = i if d_ == 0 else NT - 1 - i
                        hb = d_ * 32 + g * 4
                        so_, sn_ = Sst[d_][i % 2], Sst[d_][(i + 1) % 2]
                        P.copy("act", prevb[d_][:, c, :], so_[:], so_.r, prevb[d_].r)
                        if i == NT - 1:
                            continue
                        P.tt("pool", xd[:, d_, :].rearrange("p (r d) -> p r d", r=4),
                             xs_bf[:, c, :].rearrange("p (r d) -> p r d", r=4),
                             dtdec[:, c, hb:hb + 4].unsqueeze(2).broadcast_to([128, 4, 64]), ALU.mult,
                             xs_bf.r + dtdec.r, xd.r)
                        b = bk[5 + i % 2] if d_ == 0 else bk[2 + i % 2]
                        P.mm(b[:, 0:256], Bt[:, c, :], xd[:, d_, :], True, True, Bt.r + xd.r, b.r)
                        P.tt("dve", sn_[:].rearrange("p (r d) -> p r d", r=4),
                             so_[:].rearrange("p (r d) -> p r d", r=4),
                             decay_bc[:, c, hb:hb + 4].unsqueeze(2).broadcast_to([128, 4, 64]), ALU.mult,
                             so_.r + decay_bc.r, sn_.r)
                        P.tt("dve", sn_[:], sn_[:], b[:, 0:256], ALU.add, sn_.r + b.r, sn_.r)

                for k in range(3):
                    proj_u(k)
                    if k == 0 and pend[0] is not None:
                        pend[0]()
                        pend[0] = None
                    for t4 in range(4):
                        conv_tok(k, t4)
                    if k == 2:
                        for n in range(4):
                            conv_feat(2, n)
                for d_ in range(2):
                    P.memset("dve", Sst[d_][0][:], 0.0, Sst[d_][0].r)
                extra = [lambda: proj_u(3)] + [(lambda n=n: conv_feat(3, n)) for n in range(4)]
                for i in range(NT):
                    scan_step(i)
                    z_tile(i)
                    if i < len(extra):
                        extra[i]()
                    if i >= 12:
                        cbt_block(i - 12)
                    if 4 <= i < 14 and g + 1 < ngroups:
                        build_dg(g + 1, part=i - 4)

                def seg_head(cb, r):
                    mts = MT[r % 2]
                    for d_ in range(2):
                        head = d_ * 32 + g * 4 + r
                        sb_ = bk[3 + cnt["seg"] % 2]
                        cnt["seg"] += 1
                        P.mm(sb_[:], sel[0:64, r * 128:(r + 1) * 128],
                             acsHL[d_][0:64, cb * 512:(cb + 1) * 512], True, False, sel.r + acsHL[d_].r, sb_.r)
                        P.mm(sb_[:], identb[:], mask[d_][:].rearrange("p a b -> p (a b)"), False, True,
                             identb.r + mask[d_].r, sb_.r)
                        e_ = E[cnt["e"] % 2]
                        cnt["e"] += 1
                        for cc in range(4):
                            c = cb * 4 + cc
                            P.act(e_[:, cc * 128:(cc + 1) * 128], sb_[:, cc * 128:(cc + 1) * 128], AF.Exp,
                                  sb_.r + bias_tok.r, e_.r, bias=bias_tok[:, c, head:head + 1])
                        P.tt("pool", mts[d_][:], e_[:],
                             CBT[:, cb * 4:(cb + 1) * 4, :].rearrange("p a b -> p (a b)"), ALU.mult,
                             e_.r + CBT.r, mts[d_].r)
                    return mts

                def ydiag_head(cb, r, mts, ydb):
                    for cc in range(4):
                        c = cb * 4 + cc
                        b = ydb[cc // 2]
                        hf = cc % 2
                        o = b[:, hf * 256 + r * 64:hf * 256 + (r + 1) * 64]
                        rhs = xs_bf[:, c, r * 64:(r + 1) * 64]
                        P.mm(o, mts[0][:, cc * 128:(cc + 1) * 128], rhs, True, False, mts[0].r + xs_bf.r, b.r)
                        P.mm(o, mts[1][:, cc * 128:(cc + 1) * 128], rhs, False, False, mts[1].r + xs_bf.r, b.r)
                        P.mm(o, Dsk[:, r, :], rhs, False, True, Dsk.r + xs_bf.r, b.r)

                def combine_chunk(blk, cc):
                    cb, bi, ydb = blk
                    flush_lag()
                    yg, ssb = Yg[bi], ss[bi]
                    if cc == 0:
                        P.memset("dve", ssb[:, 0:4], 0.0, ssb.r)
                    c = cb * 4 + cc
                    hf = cc % 2
                    byd = ydb[cc // 2]
                    byo = bk[1]
                    t1 = T1[cc % 2]
                    for d_ in range(2):
                        P.mm(byo[:, d_ * 256:(d_ + 1) * 256], CT[:, c * 128:(c + 1) * 128], prevb[d_][:, c, :], True, True,
                             CT.r + prevb[d_].r, byo.r)
                    e_bc = e_tok[:, c, :].rearrange("p (d h) -> p d h", d=2)[:, :, g * 4:(g + 1) * 4]
                    P.tt("dve", t1[:].rearrange("p (d r x) -> p d r x", d=2, r=4),
                         byo[:].rearrange("p (d r x) -> p d r x", d=2, r=4),
                         e_bc.unsqueeze(3).broadcast_to([128, 2, 4, 64]), ALU.mult, byo.r + e_tok.r, t1.r)
                    Y = Yb[cc % 2]
                    P.tt("dve", Y[:], t1[:, 0:256], t1[:, 256:512], ALU.add, t1.r, Y.r)
                    P.tt("dve", Y[:], Y[:], byd[:, hf * 256:(hf + 1) * 256], ALU.add, Y.r + byd.r, Y.r)
                    P.tt("dve", yg[:, cc, :], Y[:], zs[:, c, :], ALU.mult, Y.r + zs.r[c:c + 1], yg.r)
                    lagq.append(lambda: P.act(junk[:], yg[:, cc, :], AF.Square, yg.r, junk.r + ssb.r,
                                              accum_out=ssb[:, cc:cc + 1]))

                def combine_finish(blk):
                    cb, bi, ydb = blk
                    flush_lag()
                    yg, ssb, yst, ynb = Yg[bi], ss[bi], yTst[bi], Yn[bi]
                    P.ts("dve", ssb[:, 0:4], ssb[:, 0:4], 256.0 * RMS_EPS, None, ALU.add, None, ssb.r, ssb.r)
                    rsqrt_cols(P, ssb[:, 4:8], ssb[:, 0:4], ssb[:, 8:12], ssb.r)
                    for cc in range(4):
                        P.ts("dve", ynb[:, cc, :], yg[:, cc, :], ssb[:, 4 + cc:5 + cc], None, ALU.mult, None, yg.r + ssb.r, ynb.r)
                    if pend[0] is not None:
                        pend[0]()

                    def later(ynb=ynb, yst=yst, g=g, cb=cb):
                        ptr = psTr[0]
                        for cc in range(4):
                            for k in range(2):
                                P.tr(ptr[:, k, cc, :], ynb[:, cc, k * 128:(k + 1) * 128], identb[:], ynb.r + identb.r, ptr.r)
                        for k in range(2):
                            P.ts("dve", yst[:, k, :], ptr[:, k, :, :].rearrange("p a b -> p (a b)"),
                                 normw16[:, g * 2 + k:g * 2 + k + 1], None, ALU.mult, None, ptr.r + normw16.r, yst.r)
                        P.dma("sp", yTd[g * 2:g * 2 + 2, :, cb * 512:(cb + 1) * 512].rearrange("k p t -> p k t"), yst[:],
                              yst.r, [P.r_yTd[g]])
                    pend[0] = later

                prev_blk = None
                for cb in range(4):
                    bi = cnt["blk"] % 2
                    cnt["blk"] += 1
                    ydb = (bk[5], bk[6]) if bi == 0 else (bk[0], bk[2])
                    mts_prev = None
                    for r in range(4):
                        mts = seg_head(cb, r)
                        if r > 0:
                            ydiag_head(cb, r - 1, mts_prev, ydb)
                        mts_prev = mts
                        if prev_blk is not None:
                            combine_chunk(prev_blk, r)
                    ydiag_head(cb, 3, mts_prev, ydb)
                    if prev_blk is not None:
                        combine_finish(prev_blk)
                    prev_blk = (cb, bi, ydb)
                for cc in range(4):
                    combine_chunk(prev_blk, cc)
                combine_finish(prev_blk)
            if pend[0] is not None:
                pend[0]()
                pend[0] = None
            if "yT%d" % l in P.dbg:
                t_ = P.nc.dram_tensor("dbg_yT%d" % l, [16, 128, S_TOK], BF16, kind="ExternalOutput")
                for kk in range(16):
                    P.dma("sp", t_.ap()[kk], yTd[kk], [P.r_yTd[kk // 2]], ())
            S.flush()


def merge_phase(P, l, hT, identb, ln_phase_bufs, ln_load_params, ln_tile):
    S = P.S
    din = P.din
    with ExitStack() as st:
        d = ln_phase_bufs(st)
        ln_load_params(d, 2 + 4 * l)
        wssd = P.sb(st, "wssd", [128, 16, D], BF16)
        wo = P.sb(st, "wo", [128, 8, D], BF16)
        wgs = P.sb(st, "wgs", [128, 8, D], BF16)
        bg = P.sb(st, "bg", [128, 16], F32)
        yTb = P.sb(st, "yTb", [128, 16, 512], BF16)
        gyab = P.sb(st, "gyab", [128, 8, 512], BF16)
        mT = P.sb(st, "mT", [128, 8, 512], BF16)
        gsg = [P.sb(st, "gsg%d" % i, [128, 512], F32) for i in range(2)]
        tmpf = [P.sb(st, "tmpf%d" % i, [128, 512], F32) for i in range(2)]
        hin = [P.sb(st, "hin%d" % i, [128, D], F32) for i in range(2)]
        psY = [P.ps(st, "mpsY%d" % i, [128, 512], F32) for i in range(2)]
        psG = [P.ps(st, "mpsG%d" % i, [128, 512], F32) for i in range(2)]
        psO = [P.ps(st, "mpsO%d" % i, [128, 512], F32) for i in range(2)]
        w_ssd = din["w_ssd"].ap()
        ld = P.stager(st)
        for q in range(4):
            ld(wgs[:, q * 2:(q + 1) * 2, :].rearrange("p a b -> p (a b)"), din["w_gate"].ap()[l, 1, :, q * 2048:(q + 1) * 2048], wgs.r)
        for q in range(8):
            ld(wssd[:, q * 2:(q + 1) * 2, :].rearrange("p a b -> p (a b)"), w_ssd[l, :, q * 2048:(q + 1) * 2048], wssd.r)
        for q in range(4):
            ld(wo[:, q * 2:(q + 1) * 2, :].rearrange("p a b -> p (a b)"), din["w_o"].ap()[l, :, q * 2048:(q + 1) * 2048], wo.r)
        P.dma("sp", bg[:], din["bgate"].ap()[l], (), bg.r)
        yTd = P.yTd.ap()
        gya = P.gya.ap()
        hres = P.hres.ap()
        k = 0
        pq = []
        for n in range(4):
            P.dma("sp", yTb[:], yTd[:, :, n * 512:(n + 1) * 512].rearrange("k p t -> p k t"), P.r_yTd, yTb.r)
            P.dma("sp", gyab[:], gya[:, :, n * 512:(n + 1) * 512].rearrange("k p t -> p k t"), [P.r_gya[n]], gyab.r)
            for c in range(8):
                py, pg, g_, tf = psY[k % 2], psG[k % 2], gsg[k % 2], tmpf[k % 2]
                k += 1
                for kk in range(16):
                    P.mm(py[:], wssd[:, kk, c * 128:(c + 1) * 128], yTb[:, kk, :], kk == 0, kk == 15, wssd.r + yTb.r, py.r)
                for kc in range(8):
                    P.mm(pg[:], wgs[:, kc, c * 128:(c + 1) * 128], hT[:, kc, n * 512:(n + 1) * 512], kc == 0, kc == 7,
                         wgs.r + hT.r[n * 4:n * 4 + 4], pg.r)
                P.act(g_[:], pg[:], AF.Sigmoid, pg.r + bg.r, g_.r, bias=bg[:, 8 + c:9 + c])
                P.tt("dve", tf[:], py[:], g_[:], ALU.mult, py.r + g_.r, tf.r)
                P.tt("pool", mT[:, c, :], tf[:], gyab[:, c, :], ALU.add, tf.r + gyab.r, mT.r)
            for tt in range(4):
                t = n * 4 + tt
                hi = hin[t % 2]
                P.dma("sp", hi[:], hres[t * 128:(t + 1) * 128, :], [P.r_hres[t]], hi.r)
                for half in range(2):
                    po = (psO if t % 2 == 0 else psY)[half]
                    for kc in range(8):
                        P.mm(po[:], mT[:, kc, tt * 128:(tt + 1) * 128], wo[:, kc, half * 512:(half + 1) * 512], kc == 0, kc == 7,
                             mT.r + wo.r, po.r)
                    P.stt("dve", hi[:, half * 512:(half + 1) * 512], hi[:, half * 512:(half + 1) * 512], DN_ALPHA, po[:],
                          ALU.mult, ALU.add, hi.r + po.r, hi.r)
                if len(pq) >= 2:
                    pq.pop(0)()
                pq.append(ln_tile(d, hi[:], hi.r, t, out_dram=hres[t * 128:(t + 1) * 128, :], out_res=[P.r_hres[t]]))
        for f_ in pq:
            f_()
        if "h1_%d" % l in P.dbg:
            t_ = P.nc.dram_tensor("dbg_h1_%d" % l, [S_TOK, D], F32, kind="ExternalOutput")
            for t in range(NT):
                P.dma("sp", t_.ap()[t * 128:(t + 1) * 128, :], hres[t * 128:(t + 1) * 128, :], [P.r_hres[t]], ())
        S.flush()


class _View:
    def __init__(self, ap, r):
        self.ap = ap
        self.r = r

    def __getitem__(self, k):
        return self.ap


def ffn_phase(P, l, hT, identb, ln_phase_bufs, ln_load_params, ln_tile, last):
    S = P.S
    din = P.din
    with ExitStack() as st:
        d = ln_phase_bufs(st, with_xn=False)
        ln_load_params(d, 4 + 4 * l)
        acc = P.sb(st, "acc", [128, NT, D], F32, nres=NT)
        u = [P.sb(st, "ffu%d" % i, [128, 4, S_TOK], BF16) for i in range(2)]
        w1 = [P.sb(st, "w1_%d" % i, [128, 8, 512], BF16) for i in range(2)]
        w2 = [P.sb(st, "w2_%d" % i, [128, 4, D], BF16) for i in range(2)]
        rl = [P.sb(st, "rl%d" % i, [128, 512], F32) for i in range(2)]
        psH = [P.ps(st, "psH%d" % i, [128, 512], F32) for i in range(2)]
        psO = [P.ps(st, "fpsO%d" % i, [128, 512], F32) for i in range(4)]
        ld = P.stager(st, n=2)
        w_ff1 = din["w_ff1"].ap()
        w_ff2 = din["w_ff2"].ap()
        hres = P.hres.ap()
        y_ap = P.y.ap()
        for t in range(NT):
            P.dma("sp", acc[:, t, :], hres[t * 128:(t + 1) * 128, :], [P.r_hres[t]], acc.r[t:t + 1])
            P.act(acc[:, t, :], acc[:, t, :], AF.Copy, acc.r[t:t + 1], acc.r[t:t + 1], scale=DN_ALPHA)
        cnt = {"h": 0, "o": 0}

        def load_w1(fb):
            for hh in range(2):
                ld(w1[fb % 2][:, hh * 4:(hh + 1) * 4, :].rearrange("p a b -> p (a b)"),
                   w_ff1[l, fb, :, hh * 2048:(hh + 1) * 2048], w1[fb % 2].r)

        def load_w2(fb):
            for hh in range(2):
                ld(w2[fb % 2][:, hh * 2:(hh + 1) * 2, :].rearrange("p a b -> p (a b)"),
                   w_ff2[l, fb, :, hh * 2048:(hh + 1) * 2048], w2[fb % 2].r, eng="pool")

        def stage1(fb):
            wb, ub = w1[fb % 2], u[fb % 2]
            for n in range(4):
                if n == 1 and fb + 1 < 8:
                    load_w1(fb + 1)
                for fc in range(4):
                    ph, r_ = psH[cnt["h"] % 2], rl[cnt["h"] % 2]
                    cnt["h"] += 1
                    for kc in range(8):
                        P.mm(ph[:], wb[:, kc, fc * 128:(fc + 1) * 128], hT[:, kc, n * 512:(n + 1) * 512], kc == 0, kc == 7,
                             wb.r + hT.r[n * 4:n * 4 + 4], ph.r)
                    P.act(r_[:], ph[:], AF.Relu, ph.r, r_.r)
                    P.tt("pool" if fc % 2 else "dve", ub[:, fc, n * 512:(n + 1) * 512], r_[:], r_[:], ALU.mult, r_.r, ub.r)

        pq = []

        def stage2(fb):
            wb, ub = w2[fb % 2], u[fb % 2]
            for t in range(NT):
                for half in range(2):
                    po = psO[cnt["o"] % 4]
                    cnt["o"] += 1
                    for fc in range(4):
                        P.mm(po[:], ub[:, fc, t * 128:(t + 1) * 128], wb[:, fc, half * 512:(half + 1) * 512], fc == 0, fc == 3,
                             ub.r + wb.r, po.r)
                    sl = slice(half * 512, (half + 1) * 512)
                    P.tt("dve", acc[:, t, sl], acc[:, t, sl], po[:], ALU.add, acc.r[t:t + 1] + po.r, acc.r[t:t + 1])
                if fb == 7:
                    if len(pq) >= 2:
                        pq.pop(0)()
                    v = _View(acc[:, t, :], acc.r[t:t + 1])
                    if last:
                        ln_tile(d, acc[:, t, :], acc.r[t:t + 1], t, out_dram=y_ap[t * 128:(t + 1) * 128, :], out_res=(),
                                to_hT=False, dst=v)
                    else:
                        pq.append(ln_tile(d, acc[:, t, :], acc.r[t:t + 1], t, out_dram=hres[t * 128:(t + 1) * 128, :],
                                          out_res=[P.r_hres[t]], dst=v))

        load_w1(0)
        load_w2(0)
        stage1(0)
        for fb in range(8):
            if fb + 1 < 8:
                stage1(fb + 1)
            stage2(fb)
            if fb + 1 < 8:
                load_w2(fb + 1)
        for f_ in pq:
            f_()
        S.flush()


def kernel(**inputs):
    inp = {k: np.asarray(v) for k, v in inputs.items()}
    shared = prep_shared(inp)
    P = build()
    x = np.ascontiguousarray(inp["x"], dtype=np.float32)
    in_maps = []
    for b in range(8):
        m = dict(shared)
        m["x"] = x[b]
        in_maps.append(m)
    res = run_bass_kernel_spmd(P.nc, in_maps, core_ids=list(range(8)))
    return np.stack([np.asarray(res.results[b]["y"], dtype=np.float32).reshape(S_TOK, D) for b in range(8)])
```
